# Optimizing a Trainium2 kernel written in Bass

```python
import jax, jax.numpy as jnp
from jax import lax
import numpy as np

D_MODEL = 2048
BATCH = 2
SEQ = 4096
DEPTH = 1

A_HEADS = 8
A_KEY_DIM = 128
A_VAL_DIM = 128
A_KEY_WIDTH = A_HEADS * A_KEY_DIM
A_WIDTH = A_HEADS * A_VAL_DIM
A_CHUNK = 32
B_GROUPS = 8
B_GROUP_DIM = 128
B_WIDTH = B_GROUPS * B_GROUP_DIM
B_CHUNK = 128
D_FF = 4 * D_MODEL
N_MOD = 6
EPS = 1e-6

IN_SIZES = (A_KEY_WIDTH, A_KEY_WIDTH, A_KEY_WIDTH, A_WIDTH, A_WIDTH, 2 * B_WIDTH, D_MODEL, D_MODEL)
IN_WIDTH = sum(IN_SIZES)
IN_SPLIT_POINTS = tuple(int(s) for s in np.cumsum(IN_SIZES)[:-1])

kernel_name = 'hybrid_hgrn2_sgu_block'


def rms_norm(x, g):
    xf = x.astype(jnp.float32)
    y = xf * lax.rsqrt(jnp.mean(xf * xf, axis=-1, keepdims=True) + EPS)
    return (y * g.astype(jnp.float32)).astype(x.dtype)


def layer_norm(x, g):
    xf = x.astype(jnp.float32)
    mu = jnp.mean(xf, axis=-1, keepdims=True)
    d = xf - mu
    y = d * lax.rsqrt(jnp.mean(d * d, axis=-1, keepdims=True) + EPS)
    return (y * g.astype(jnp.float32)).astype(x.dtype)


def to_heads(t, d):
    b, l, _ = t.shape
    return t.reshape(b, l, A_HEADS, d).transpose(0, 2, 1, 3)


def gated_state_scan(q, k, v, log_f):
    bn, h, l, dk = q.shape
    dv = v.shape[-1]
    n = l // A_CHUNK
    rs = lambda t: t.reshape(bn, h, n, A_CHUNK, t.shape[-1])
    q, k, v, log_f = rs(q), rs(k), rs(v), rs(log_f)
    b = jnp.cumsum(log_f, axis=3)
    b_last = b[:, :, :, -1:, :]
    q_dec = q * jnp.exp(b)
    k_dec = k * jnp.exp(-b)
    k_end = k * jnp.exp(b_last - b)
    mask = jnp.tril(jnp.ones((A_CHUNK, A_CHUNK), dtype=bool))
    att = jnp.einsum('bhnck,bhnsk->bhncs', q_dec, k_dec)
    att = jnp.where(mask, att, 0.0)
    o_intra = jnp.einsum('bhncs,bhnsv->bhncv', att, v)
    decay = jnp.exp(b_last[:, :, :, 0, :])

    def step(state, xs):
        qd, ke, vc, dc = xs
        o = jnp.einsum('bhck,bhkv->bhcv', qd, state)
        state = dc[..., None] * state + jnp.einsum('bhck,bhcv->bhkv', ke, vc)
        return state, o

    xs = (jnp.moveaxis(q_dec, 2, 0), jnp.moveaxis(k_end, 2, 0),
          jnp.moveaxis(v, 2, 0), jnp.moveaxis(decay, 2, 0))
    s0 = jnp.zeros((bn, h, dk, dv), jnp.float32)
    _, o_inter = lax.scan(step, s0, xs)
    o = o_intra + jnp.moveaxis(o_inter, 0, 2)
    return o.reshape(bn, h, l, dv)


def hgrn2_forget(logit, lb):
    lf = logit.astype(jnp.float32)
    log_f = jnp.log(lb + (1.0 - lb) * jax.nn.sigmoid(lf))
    k = (1.0 - lb) * jax.nn.sigmoid(-lf)
    return k, log_f


def hgrn2_bidir(q, f_fw, f_bw, i_v, o_gate, lb_fw, lb_bw, g_norm):
    bn, l, _ = q.shape
    k_fw, lf_fw = hgrn2_forget(f_fw, lb_fw)
    k_bw, lf_bw = hgrn2_forget(f_bw, lb_bw)
    qh = to_heads(q.astype(jnp.float32) * (A_KEY_DIM ** -0.5), A_KEY_DIM)
    vh = to_heads(i_v.astype(jnp.float32), A_VAL_DIM)
    o_fw = gated_state_scan(qh, to_heads(k_fw, A_KEY_DIM), vh, to_heads(lf_fw, A_KEY_DIM))
    fl = lambda t: jnp.flip(t, axis=2)
    o_bw = fl(gated_state_scan(fl(qh), fl(to_heads(k_bw, A_KEY_DIM)), fl(vh),
                               fl(to_heads(lf_bw, A_KEY_DIM))))
    o = rms_norm(o_fw + o_bw, g_norm)
    o = o.transpose(0, 2, 1, 3).reshape(bn, l, A_WIDTH).astype(o_gate.dtype)
    return o * jax.nn.silu(o_gate)


def chunked_sgu(z, g_v, w_s, b_s):
    bn, l, _ = z.shape
    z = jax.nn.gelu(z, approximate=False)
    u, v = jnp.split(z, 2, axis=-1)
    v = layer_norm(v, g_v)
    v = v.reshape(bn, l // B_CHUNK, B_CHUNK, B_GROUPS, B_GROUP_DIM)
    vm = jnp.einsum('gts,bnsgc->bntgc', w_s, v) + b_s.T[None, None, :, :, None]
    return u * vm.reshape(bn, l, B_WIDTH)


def setup_inputs(seed: int = 0) -> dict:
    key = jax.random.key(seed)
    ks = jax.random.split(key, 20)
    nrm = lambda k, shape, s: jax.random.normal(k, shape, jnp.float32) * s
    gain = lambda k, shape: 1.0 + nrm(k, shape, 0.02)
    return {
        'x': nrm(ks[0], (BATCH, SEQ, D_MODEL), 1.0),
        'c': nrm(ks[1], (BATCH, D_MODEL), 1.0),
        'w_ada': nrm(ks[2], (DEPTH, D_MODEL, N_MOD * D_MODEL), D_MODEL ** -0.5),
        'b_ada': nrm(ks[3], (DEPTH, N_MOD * D_MODEL), 0.02),
        'g_pre_mix': gain(ks[4], (DEPTH, D_MODEL)),
        'g_post_mix': gain(ks[5], (DEPTH, D_MODEL)),
        'g_pre_ffn': gain(ks[6], (DEPTH, D_MODEL)),
        'g_post_ffn': gain(ks[7], (DEPTH, D_MODEL)),
        'w_in': nrm(ks[8], (DEPTH, D_MODEL, IN_WIDTH), D_MODEL ** -0.5),
        'lb_logits': nrm(ks[9], (2, DEPTH + 1, A_KEY_WIDTH), 0.1),
        'g_hgrn_norm': gain(ks[10], (DEPTH, A_VAL_DIM)),
        'w_a_out': nrm(ks[11], (DEPTH, A_WIDTH, D_MODEL), A_WIDTH ** -0.5),
        'g_sgu_norm': gain(ks[12], (DEPTH, B_WIDTH)),
        'w_spatial': nrm(ks[13], (DEPTH, B_GROUPS, B_CHUNK, B_CHUNK), B_CHUNK ** -0.5),
        'b_spatial': nrm(ks[14], (DEPTH, B_GROUPS, B_CHUNK), 0.02),
        'w_b_out': nrm(ks[15], (DEPTH, B_WIDTH, D_MODEL), B_WIDTH ** -0.5),
        'w_o': nrm(ks[16], (DEPTH, D_MODEL, D_MODEL), D_MODEL ** -0.5),
        'w_ff1': nrm(ks[17], (DEPTH, D_MODEL, D_FF), D_MODEL ** -0.5),
        'w_ff2': nrm(ks[18], (DEPTH, D_FF, D_MODEL), D_FF ** -0.5),
    }


def reference(x, c, w_ada, b_ada, g_pre_mix, g_post_mix, g_pre_ffn, g_post_ffn, w_in,
              lb_logits, g_hgrn_norm, w_a_out, g_sgu_norm, w_spatial, b_spatial,
              w_b_out, w_o, w_ff1, w_ff2):
    lb_all = jnp.cumsum(jax.nn.softmax(lb_logits.astype(jnp.float32), axis=1), axis=1)
    h = x
    for l in range(DEPTH):
        mod = jax.nn.silu(c) @ w_ada[l] + b_ada[l]
        sh1, sc1, gt1, sh2, sc2, gt2 = [m[:, None, :] for m in jnp.split(mod, N_MOD, axis=-1)]
        a = rms_norm(h, g_pre_mix[l]) * (1 + sc1) + sh1
        proj = a @ w_in[l]
        q, f_fw, f_bw, i_v, o_gate, z, gate_a, gate_b = jnp.split(proj, IN_SPLIT_POINTS, axis=-1)
        y_a = hgrn2_bidir(q, f_fw, f_bw, i_v, o_gate, lb_all[0, l], lb_all[1, l],
                          g_hgrn_norm[l]) @ w_a_out[l]
        y_b = chunked_sgu(z, g_sgu_norm[l], w_spatial[l], b_spatial[l]) @ w_b_out[l]
        merged = jax.nn.sigmoid(gate_a) * y_a + jax.nn.sigmoid(gate_b) * y_b
        h = h + gt1 * rms_norm(merged @ w_o[l], g_post_mix[l])
        a = rms_norm(h, g_pre_ffn[l]) * (1 + sc2) + sh2
        ff = jnp.square(jax.nn.relu(a @ w_ff1[l])) @ w_ff2[l]
        h = h + gt2 * rms_norm(ff, g_post_ffn[l])
    return h
```

```python
from contextlib import ExitStack
import numpy as np
import concourse.bass as bass
import concourse.mybir as mybir
from concourse.bass_utils import run_bass_kernel_spmd

F32 = mybir.dt.float32
BF16 = mybir.dt.bfloat16
AF = mybir.ActivationFunctionType
ALU = mybir.AluOpType

P = 128
T = 1024
D = 2048
KC = 16
NT = 8
H = 8
CH = 64
NCH = T // CH
DFF = 8192
EPS = 1e-6
INW = 11264
C_Q, C_FF, C_FB, C_V, C_OG, C_ZU, C_ZV, C_GA, C_GB = 0, 1024, 2048, 3072, 4096, 5120, 6144, 7168, 9216
SW = 2 * P + 2
GROUPS8 = [[0, 1, 2, 3], [4, 5, 6, 7]]


class Sched:
    ENGS = ("pe", "act", "dve", "pool", "sp")

    def __init__(self):
        self.ops = {e: [] for e in self.ENGS}
        self.last_w = {}
        self.readers = {}
        self.dma_cnt = {}
        self.known = {e: {} for e in self.ENGS}
        self.signal = {e: set() for e in self.ENGS}

    def _need(self, eng, tok, needs, war=False):
        if tok is None:
            return
        if tok[0] == eng and (eng == "pe" or war):
            return
        needs.append(tok)

    def _waits(self, eng, needs):
        best = {}
        for kind, idx in needs:
            if idx > best.get(kind, -1):
                best[kind] = idx
        waits = []
        for kind, idx in best.items():
            if self.known[eng].get(kind, -1) >= idx:
                continue
            self.known[eng][kind] = idx
            waits.append((kind, idx))
            if kind in self.ENGS:
                self.signal[kind].add(idx)
        return waits

    def op(self, eng, fn, r=(), w=(), dma=None):
        needs = []
        for k in r:
            self._need(eng, self.last_w.get(k), needs)
        for k in w:
            self._need(eng, self.last_w.get(k), needs, war=True)
            for t in self.readers.get(k, ()):
                self._need(eng, t, needs, war=True)
        waits = self._waits(eng, needs)
        if dma is None:
            tok = (eng, len(self.ops[eng]))
        else:
            n = self.dma_cnt.get(dma, 0)
            self.dma_cnt[dma] = n + 1
            tok = ("dma:" + dma, n)
        self.ops[eng].append((fn, waits, dma))
        for k in r:
            self.readers.setdefault(k, []).append(tok)
        for k in w:
            self.last_w[k] = tok
            self.readers[k] = []
        return tok

    def barrier(self):
        toks = []
        for e in self.ENGS:
            for i in range(len(self.ops[e]) - 1, -1, -1):
                if self.ops[e][i][0] is not None and self.ops[e][i][2] is None:
                    toks.append((e, i))
                    break
        for k, n in self.dma_cnt.items():
            toks.append(("dma:" + k, n - 1))
        for e in self.ENGS:
            needs = [t for t in toks if t[0] != e or e in ("act", "dve", "pool")]
            self.ops[e].append((None, self._waits(e, needs), None))

    def emit(self, nc, block, stack, final_tokens):
        sems = {}
        for e in self.ENGS:
            sems[e] = stack.enter_context(nc.semaphore("sem_" + e))
        for k in self.dma_cnt:
            sems["dma:" + k] = stack.enter_context(nc.semaphore("sd_" + k))
        rank = {}
        for e in self.ENGS:
            rank[e] = {idx: i + 1 for i, idx in enumerate(sorted(self.signal[e]))}
            assert len(rank[e]) < 60000, (e, len(rank[e]))

        def val(kind, idx):
            if kind in self.ENGS:
                return rank[kind][idx]
            if kind.startswith("dma:cc"):
                return idx + 1
            return 16 * (idx + 1)

        def run(ename, eobj, extra_final):
            for i, (fn, waits, dma) in enumerate(self.ops[ename]):
                for kind, idx in waits:
                    eobj.wait_ge(sems[kind], val(kind, idx))
                if fn is None:
                    continue
                ins = fn(eobj)
                if dma is not None:
                    if dma.startswith("cc"):
                        ins.then_inc(sems["dma:" + dma])
                    else:
                        ins.then_inc(sems["dma:" + dma], 16)
                elif i in self.signal[ename]:
                    ins.then_inc(sems[ename], 1)
            if extra_final:
                for kind, idx in final_tokens:
                    eobj.wait_ge(sems[kind], val(kind, idx))

        @block.tensor
        def _(e):
            run("pe", e, False)

        @block.scalar
        def _(e):
            run("act", e, False)

        @block.vector
        def _(e):
            run("dve", e, False)

        @block.gpsimd
        def _(e):
            run("pool", e, False)

        @block.sync
        def _(e):
            run("sp", e, True)


def build(stage=99, dbg=(), ncores=8, nocc=False, sub=99):
    nc = bass.Bass("TRN2", target_bir_lowering=False)
    stack = ExitStack()
    S = Sched()
    GROUPS = GROUPS8 if ncores == 8 else [[0, 1, 2, 3]]

    def din(name, shape, dt=F32):
        return nc.dram_tensor(name, list(shape), dt, kind="ExternalInput").ap()

    x_d = din("x", [T, D])
    cT_d = din("cT", [P, KC])
    wada_d = din("w_ada", [D, 3072])
    bada_d = din("b_ada", [1, 3072])
    rows_d = din("rows", [P, P])
    rows2_d = din("rows2", [P, P])
    gpost_d = din("gpost", [2, D])
    gsgu_d = din("gsgu", [1, 1024])
    bsp_d = din("bsp", [1, 1024])
    wsp_d = din("wsp", [H, P, P])
    win_d = din("w_in", [D, INW] if stage >= 2 else [1, 1])
    waout_d = din("w_a_out", [1024, D] if stage >= 4 else [1, 1])
    wbout_d = din("w_b_out", [1024, D] if stage >= 4 else [1, 1])
    wo_d = din("w_o", [D, D] if stage >= 4 else [1, 1])
    wff1_d = din("w_ff1", [D, DFF] if stage >= 5 else [1, 1])
    wff2_d = din("w_ff2", [DFF, D] if stage >= 5 else [1, 1])
    cst_d = din("cst", [P, 4 * P])
    posm_d = din("posm", [P, 8])
    out_d = nc.dram_tensor("out", [T, D], F32, kind="ExternalOutput").ap()
    dbg_d = {}
    for name, shape in dbg:
        dbg_d[name] = nc.dram_tensor("dbg_" + name, list(shape), F32, kind="ExternalOutput").ap()

    modb_d = nc.dram_tensor("mod_bounce", [1, 3072], F32).ap()
    modg_d = nc.dram_tensor("mod_gath", [4, 3072], F32).ap()
    stb_d = [nc.dram_tensor(f"st_bounce{h}", [P, SW], F32).ap() for h in range(H)]
    stg_d = [nc.dram_tensor(f"st_gath{h}", [4 * P, SW], F32).ap() for h in range(H)]

    def sb(name, shape, dt=F32):
        return stack.enter_context(nc.sbuf_tensor("s_" + name, list(shape), dt))

    ps = stack.enter_context(nc.psum_tensor("ps", [P, 8, 512], F32))

    def bank(b):
        return ps[:, b, :]

    def bankbf(b):
        return ps[:, b, :].bitcast(BF16)

    def PK(b):
        return ("ps", b)

    cst = sb("cst", [P, 4 * P])
    ident = cst[:, 0:P]
    mask = [cst[:, P:2 * P], cst[:, 2 * P:3 * P]]
    onesm = cst[:, 3 * P:4 * P]
    identb = sb("identb", [P, P], BF16)
    ones1 = sb("ones1", [1, P])
    posm = sb("posm", [P, 16])
    modT = sb("modT", [P, P])
    gm = sb("gm", [P, 32])
    r2T = sb("r2T", [P, P])
    lbv = sb("lbv", [P, 64])
    gsg = sb("gsg", [P, 1024])
    bsp = sb("bsp", [1, 1024])
    wsT = sb("wsT", [P, H, P], BF16)
    zeros = sb("zeros", [P, 1024])
    aT = sb("aT", [P, KC, T], BF16)
    NW = 3
    wb = [sb(f"wb{i}", [P, KC, 512], BF16) for i in range(NW)]
    st = sb("stat", [P, 64])
    ARENA_W = 26500
    arena = sb("arena", [P, ARENA_W])

    def cf(off, n):
        return arena[:, off:off + n]

    def cb(off_words, n):
        return arena[:, off_words:off_words + n // 2].bitcast(BF16)

    wcount = [0]
    NWc = [NW]

    def wslot():
        i = wcount[0] % NWc[0]
        wcount[0] += 1
        return i

    def pool_dma(dst, src, i):
        S.op("pool", lambda e: e.dma_start(out=dst, in_=src), r=[], w=[("wb", i)], dma=f"wb{i}")

    def load_w(src_ap, kcs=KC, cols=512):
        i = wslot()
        pool_dma(wb[i][:, 0:kcs, 0:cols], src_ap.rearrange("(kc p) c -> p kc c", p=P), i)
        return i

    def sp_dma(out, in_, r, w, key):
        return S.op("sp", lambda e: e.dma_start(out=out, in_=in_), r=r, w=w, dma=key)

    def mm(out, lhsT, rhs, start, stop, r, w):
        S.op("pe", lambda e: e.matmul(out, lhsT, rhs, start=start, stop=stop, skip_group_check=True), r=r, w=w)

    def tr(out, in_, idn, r, w):
        S.op("pe", lambda e: e.transpose(out, in_, idn), r=r, w=w)

    def act(out, in_, func, r, w, bias=None, scale=None, accum=None):
        kw = {}
        if bias is not None:
            kw["bias"] = bias
        if scale is not None:
            kw["scale"] = scale
        if accum is not None:
            kw["accum_out"] = accum
        S.op("act", lambda e: e.activation(out, in_, func, **kw), r=r, w=w)

    def ts(out, in0, s1, s2, op0, op1=None, r=(), w=(), eng="dve"):
        if op1 is None:
            S.op(eng, lambda e: e.tensor_scalar(out, in0, s1, None, op0), r=r, w=w)
        else:
            S.op(eng, lambda e: e.tensor_scalar(out, in0, s1, s2, op0, op1), r=r, w=w)

    def tt(out, in0, in1, op, r, w, eng="dve"):
        S.op(eng, lambda e: e.tensor_tensor(out, in0, in1, op), r=r, w=w)

    def stt(out, in0, scalar, in1, op0, op1, r, w):
        S.op("dve", lambda e: e.scalar_tensor_tensor(out, in0, scalar, in1, op0, op1), r=r, w=w)

    def cp(out, in_, r, w, eng="dve"):
        if eng == "act":
            S.op("act", lambda e: e.copy(out, in_), r=r, w=w)
        else:
            S.op(eng, lambda e: e.tensor_copy(out, in_), r=r, w=w)

    pb = [0]
    pool_banks = [list(range(8))]

    def nb():
        lst = pool_banks[0]
        b = lst[pb[0] % len(lst)]
        pb[0] += 1
        return b

    finals = []

    def dump(name, src_ap, rkeys, rows=None):
        if name in dbg_d:
            dst = dbg_d[name] if rows is None else dbg_d[name][rows[0]:rows[1], :]
            finals.append(sp_dma(dst, src_ap, r=rkeys, w=[("dbg", name, rows)], key="dbg_" + name))

    def dump_bf(name, src_bf_ap, rkeys, n, rows, scratch_off):
        if name in dbg_d:
            tmp = cf(scratch_off, n)
            cp(tmp, src_bf_ap, rkeys, [("dbgtmp", scratch_off)])
            finals.append(sp_dma(dbg_d[name][rows[0]:rows[1], :], tmp, [("dbgtmp", scratch_off)],
                                 [("dbg", name, rows)], "dbg_" + name))

    def finish():
        with nc.Block() as block:
            S.emit(nc, block, stack, finals)
        return nc, stack

    sp_dma(cst[:], cst_d, [], ["cst"], "c0")
    sp_dma(posm[:, 0:8], posm_d, [], ["posm"], "c1")
    rows = cf(0, P)
    rows2 = cf(P, P)
    cTs = cf(2 * P, KC)
    sp_dma(rows[96:128, :], rows_d[96:128, :], [], ["rows_hi"], "c2")
    sp_dma(rows2, rows2_d, [], ["rows2"], "c3")
    sp_dma(cTs, cT_d, [], ["cTs"], "c4")
    sp_dma(bsp[:], bsp_d, [], ["bsp"], "c5")
    sp_dma(gsg[:], gsgu_d.partition_broadcast(P), [], ["gsg"], "c6")
    badas = cf(3 * P, 3072)
    sp_dma(badas[0:1, :], bada_d, [], ["badas"], "c7")
    S.op("dve", lambda e: e.memset(ones1[:], 1.0), r=[], w=["ones1"])
    S.op("dve", lambda e: e.memset(zeros[:], 0.0), r=[], w=["zeros"])
    epsc = st[:, 63:64]
    S.op("dve", lambda e: e.memset(epsc, EPS), r=[], w=["epsc"])
    cp(identb[:], ident, ["cst"], ["identb"])
    ts(posm[:, 8:16], posm[:, 0:8], -1.0, 1.0, ALU.mult, ALU.add, r=["posm"], w=["posm"])

    csig = cf(3 * P + 3072, KC)
    act(csig, cTs, AF.Sigmoid, ["cTs"], ["csig"])
    csb = cb(3 * P + 3072 + KC, 2 * KC)[:, 0:KC]
    tt(csb, csig, cTs, ALU.mult, ["csig", "cTs"], ["csb"])
    modp = cf(3 * P + 3072 + 2 * KC, 3072)
    for j in range(6):
        wi = load_w(wada_d[:, j * 512:(j + 1) * 512])
        b = nb()
        for kc in range(KC):
            mm(bank(b)[0:1, :], csb[:, kc:kc + 1], wb[wi][:, kc, :], kc == 0, kc == KC - 1,
               r=["csb", ("wb", wi)], w=[PK(b)])
        tt(modp[0:1, j * 512:(j + 1) * 512], bank(b)[0:1, :], badas[0:1, j * 512:(j + 1) * 512], ALU.add,
           [PK(b), "badas"], [PK(b), "modp"])
    sp_dma(modb_d, modp[0:1, :], ["modp"], ["modb"], "c8")
    def allgather(src_d, dst_d, rkeys, wkeys, key, nrows):
        if nocc:
            S.op("pool", lambda e: e.dma_start(out=dst_d[0:nrows, :], in_=src_d), r=rkeys, w=wkeys, dma="nocc_" + key)
        else:
            S.op("pool", lambda e: e.collective_compute("AllGather", ALU.bypass, replica_groups=GROUPS,
                                                        ins=[src_d.opt()], outs=[dst_d.opt()]),
                 r=rkeys, w=wkeys, dma="cc_" + key)
    allgather(modb_d, modg_d, ["modb"], ["modg"], "mod", 1)
    sp_dma(rows[0:96, :], modg_d.rearrange("r (j c) -> (r j) c", c=P), ["modg"], ["rows_lo"], "c9")
    b = nb()
    tr(bank(b)[:, 0:P], rows, ident, ["rows_lo", "rows_hi", "cst"], [PK(b)])
    cp(modT[:], bank(b)[:, 0:P], [PK(b)], [PK(b), "modT"])
    stt(gm[:, 0:16], modT[:, 16:32], 1.0, modT[:, 96:112], ALU.add, ALU.mult, ["modT"], ["gm"])
    stt(gm[:, 16:32], modT[:, 64:80], 1.0, modT[:, 112:128], ALU.add, ALU.mult, ["modT"], ["gm"])
    b = nb()
    tr(bank(b)[:, 0:P], rows2, ident, ["rows2", "cst"], [PK(b)])
    cp(r2T[:], bank(b)[:, 0:P], [PK(b)], [PK(b), "r2T"])
    for d in range(2):
        tt(lbv[:, 16 * d:16 * d + 8], r2T[:, 16 * d:16 * d + 8], r2T[:, 16 * d + 8:16 * d + 16], ALU.subtract,
           ["r2T"], ["lbv"])
        act(lbv[:, 16 * d:16 * d + 8], lbv[:, 16 * d:16 * d + 8], AF.Sigmoid, ["lbv"], ["lbv"])
        ts(lbv[:, 16 * d + 8:16 * d + 16], lbv[:, 16 * d:16 * d + 8], -1.0, 1.0, ALU.mult, ALU.add, r=["lbv"], w=["lbv"])
        ts(lbv[:, 32 + 16 * d:32 + 16 * d + 8], lbv[:, 16 * d + 8:16 * d + 16], -1.0, None, ALU.mult, r=["lbv"], w=["lbv"])
    wsl = cf(8192, H * P)
    sp_dma(wsl.rearrange("p (g s) -> p g s", g=H), wsp_d.rearrange("g t s -> t g s"), [], ["wsl"], "c14")
    for g in range(H):
        b = nb()
        tr(bank(b)[:, 0:P], wsl[:, g * P:(g + 1) * P], ident, ["wsl", "cst"], [PK(b)])
        cp(wsT[:, g, :], bank(b)[:, 0:P], [PK(b)], [PK(b), "wsT"])
    dump("modT", modT[:], ["modT"])
    dump("lbv", lbv[:], ["lbv"])
    S.barrier()

    def norm_to_aT(get_tile, junk_off, which):
        gmc = gm[:, 16 * which:16 * which + 16]
        shc = modT[:, 48 * which:48 * which + 16]
        for g4 in range(2):
            tiles = []
            for q in range(4):
                t_i = g4 * 4 + q
                xt, keys = get_tile(t_i)
                sq = cb(junk_off, D)
                sc = st[:, 2 * t_i:2 * t_i + 1]
                act(sq, xt, AF.Square, keys, [("junk",), ("st", t_i)], accum=sc)
                rs = st[:, 2 * t_i + 1:2 * t_i + 2]
                act(rs, sc, AF.Ln, [("st", t_i)], [("st", t_i)], bias=epsc, scale=1.0 / D)
                act(rs, rs, AF.Exp, [("st", t_i)], [("st", t_i)], scale=-0.5)
                ts(xt, xt, rs, None, ALU.mult, r=keys + [("st", t_i)], w=keys)
                tiles.append((xt, keys))
            for kc in range(KC):
                b = nb()
                for q in range(4):
                    xt, keys = tiles[q]
                    tr(bank(b)[:, q * P:(q + 1) * P], xt[:, kc * P:(kc + 1) * P], ident, keys + ["cst"], [PK(b)])
                act(aT[:, kc, g4 * 512:(g4 + 1) * 512], bank(b), AF.Identity, [PK(b), "gm", "modT"],
                    [PK(b), ("aT", kc, g4)], bias=shc[:, kc:kc + 1], scale=gmc[:, kc:kc + 1])

    def get_x(t_i):
        q = t_i % 4
        xt = cf(q * D, D)
        key = ("xt", q)
        sp_dma(xt, x_d[t_i * P:(t_i + 1) * P, :], [], [key], f"xt{q}")
        return xt, [key]
    norm_to_aT(get_x, 4 * D, 0)
    if "aT" in dbg_d:
        for kc in range(KC):
            dump_bf("aT", aT[:, kc, :], [("aT", kc, 0), ("aT", kc, 1)], T, (kc * P, (kc + 1) * P), 4 * D + 1024 + (kc % 2) * T)
    if stage < 2:
        return finish()
    S.barrier()

    aT_all = [("aT", kc, g4) for kc in range(KC) for g4 in range(2)]

    def proj_fm(wi, hh, half, b):
        for kc in range(KC):
            mm(bank(b), wb[wi][:, kc, hh * P:(hh + 1) * P], aT[:, kc, half * 512:(half + 1) * 512],
               kc == 0, kc == KC - 1, r=[("wb", wi), ("aT", kc, half)], w=[PK(b)])

    def proj_tm(wi, t_i, b):
        for kc in range(KC):
            mm(bank(b), aT[:, kc, t_i * P:(t_i + 1) * P], wb[wi][:, kc, :],
               kc == 0, kc == KC - 1, r=[("wb", wi), ("aT", kc, t_i // 4)], w=[PK(b)])

    A_VTM = 0
    A_HG = 4096
    A_QT = 8192
    A_SOG = A_QT + 1024
    A_WK = A_SOG + 1024
    WKSZ = 4 * 1024 + 8
    A_PERS = A_WK + WKSZ
    PSZ = 3 * 512 + 16
    A_KG = A_PERS + 4 * PSZ
    A_STL = A_KG + 3 * 512
    A_REC = A_STL + 264
    R_GST = A_REC
    R_S32 = R_GST + 4 * SW
    R_SBF = R_S32 + 2 * P
    R_ATT = R_SBF + 2 * 64
    R_TMP = R_ATT + 2 * 64
    R_DEF = R_TMP + P
    R_SQ = R_DEF + 8
    R_RS = R_SQ + 1024
    A_END = R_RS + 1024
    assert A_END <= ARENA_W, A_END
    vtm = cb(A_VTM, NT * 1024).rearrange("p (t c) -> p t c", t=NT)
    hgT = cb(A_HG, H * 1024).rearrange("p (h c) -> p h c", h=H)
    pool_banks[0] = [0, 1, 2, 3]
    B_ATT, B_U, B_O = 4, 5, 6

    for cg in range(2):
        wi = load_w(win_d[:, C_V + cg * 512:C_V + (cg + 1) * 512])
        for t_i in range(NT):
            b = nb()
            proj_tm(wi, t_i, b)
            cp(vtm[:, t_i, cg * 512:(cg + 1) * 512], bank(b), [PK(b)], [PK(b), ("vtm", t_i)],
               eng="act" if t_i % 2 else "dve")

    def pers(par, d):
        base = A_PERS + (par * 2 + d) * PSZ
        return cb(base, 1024), cb(base + 512, 1024), cb(base + 1024, 1024), cf(base + 1536, 16)

    def head_prep(h):
        par = h % 2
        wi = wslot()
        for j, c0 in enumerate((C_Q, C_FF, C_FB, C_OG)):
            pool_dma(wb[wi][:, :, j * P:(j + 1) * P],
                     win_d[:, c0 + h * P:c0 + (h + 1) * P].rearrange("(kc p) c -> p kc c", p=P), wi)
        qT = cf(A_QT, 1024)
        sog = cb(A_SOG + par * 512, 1024)
        stl = cf(A_STL, SW)
        for half in range(2):
            b = nb()
            proj_fm(wi, 0, half, b)
            S.op("act", lambda e, b=b, half=half: e.mul(qT[:, half * 512:(half + 1) * 512], bank(b), float(P) ** -0.5),
                 r=[PK(b)], w=[PK(b), ("qT",)])
        sgt = cf(A_WK, 1024)
        for half in range(2):
            b = nb()
            proj_fm(wi, 3, half, b)
            hs = slice(half * 512, (half + 1) * 512)
            act(sgt[:, hs], bank(b), AF.Sigmoid, [PK(b)], [PK(b), ("wk",)])
            tt(sog[:, hs], bank(b), sgt[:, hs], ALU.mult, [PK(b), ("wk",)], [PK(b), ("sog", par)])
        for d in range(2):
            sig = cf(A_WK, 1024)
            lfb = cf(A_WK + 1024, 1024)
            Bext = cf(A_WK + 2048, 1032)
            E1 = cf(A_WK + 2048 + 1032, 1024)
            kW = ("wk",)
            qd, kd, ketm, dcs = pers(par, d)
            kP = ("pers", par, d)
            lb_c = lbv[:, 16 * d + h:16 * d + h + 1]
            oml_c = lbv[:, 16 * d + 8 + h:16 * d + 8 + h + 1]
            noml_c = lbv[:, 32 + 16 * d + h:32 + 16 * d + h + 1]
            for half in range(2):
                b = nb()
                proj_fm(wi, 1 + d, half, b)
                act(sig[:, half * 512:(half + 1) * 512], bank(b), AF.Sigmoid, [PK(b)], [PK(b), kW])
            act(lfb, sig, AF.Ln, [kW, "lbv"], [kW], bias=lb_c, scale=oml_c)
            ts(sig, sig, noml_c, oml_c, ALU.mult, ALU.add, r=[kW, "lbv"], w=[kW])
            S.op("dve", lambda e, Bext=Bext: e.memset(Bext[:, 0:8], 0.0), r=[], w=[kW])
            S.op("dve", lambda e, Bext=Bext, lfb=lfb: e.tensor_tensor_scan(Bext[:, 8:1032], lfb, zeros[:], 0.0, ALU.add, ALU.add),
                 r=[kW, "zeros"], w=[kW])
            Bin = Bext[:, 8:1032]
            Bex = Bext[:, 7:1031]
            B3 = Bin.rearrange("p (n c) -> p n c", c=CH)
            lf3 = lfb.rearrange("p (n c) -> p n c", c=CH)
            if d == 0:
                Bs = Bex.rearrange("p (n c) -> p n c", c=CH)[:, :, 0:1].to_broadcast([P, NCH, CH])
                tt(lf3, B3, Bs, ALU.subtract, [kW], [kW])
            else:
                Be = B3[:, :, CH - 1:CH].to_broadcast([P, NCH, CH])
                tt(lf3, lf3, Be, ALU.add, [kW], [kW])
                tt(lfb, lfb, Bin, ALU.subtract, [kW], [kW])
            act(E1, lfb, AF.Exp, [kW], [kW])
            act(lfb, lfb, AF.Exp, [kW], [kW], scale=-1.0)
            tt(qd, cf(A_QT, 1024), E1, ALU.mult, [kW, ("qT",)], [kP])
            tt(lfb, sig, lfb, ALU.mult, [kW], [kW])
            cp(kd, lfb, [kW], [kP])
            E13 = E1.rearrange("p (n c) -> p n c", c=CH)
            last = CH - 1 if d == 0 else 0
            cp(dcs, E13[:, :, last], [kW], [kP])
            kefm = cb(A_KG + 1024, 1024)
            tt(kefm.rearrange("p (n c) -> p n c", c=CH), lf3, E13[:, :, last:last + 1].to_broadcast([P, NCH, CH]),
               ALU.mult, [kW], [("kefm",)])
            kgfm = cb(A_KG, 1024)
            kgtm = cb(A_KG + 512, 1024)
            act(stl[:, 2 * P + d:2 * P + d + 1], Bext[:, 1031:1032], AF.Exp, [kW], [("stl",)])
            if d == 0:
                ts(st[:, 40:41], Bext[:, 1031:1032], 1.0, None, ALU.mult, r=[kW], w=[("btot",)])
                act(E1, Bin, AF.Exp, [kW, ("btot",)], [kW], bias=st[:, 40:41], scale=-1.0)
            else:
                act(E1, Bex, AF.Exp, [kW], [kW])
            tt(kgfm, sig, E1, ALU.mult, [kW], [("kgfm",)])
            b1 = nb()
            b2 = nb()
            for t_i in range(NT):
                tr(bankbf(b1)[:, t_i * P:(t_i + 1) * P], kefm[:, t_i * P:(t_i + 1) * P], identb[:], [("kefm",), "identb"], [PK(b1)])
                tr(bankbf(b2)[:, t_i * P:(t_i + 1) * P], kgfm[:, t_i * P:(t_i + 1) * P], identb[:], [("kgfm",), "identb"], [PK(b2)])
            cp(ketm, bankbf(b1), [PK(b1)], [PK(b1), kP], eng="act")
            cp(kgtm, bankbf(b2), [PK(b2)], [PK(b2), ("kgtm",)])
            b3 = nb()
            for t_i in range(NT):
                mm(bank(b3)[:, 0:P], kgtm[:, t_i * P:(t_i + 1) * P], vtm[:, t_i, h * P:(h + 1) * P], t_i == 0, t_i == NT - 1,
                   r=[("kgtm",), ("vtm", t_i)], w=[PK(b3)])
            cp(stl[:, d * P:(d + 1) * P], bank(b3)[:, 0:P], [PK(b3)], [PK(b3), ("stl",)])
        sp_dma(stb_d[h], stl, [("stl",)], [("stb", h)], f"stb{par}")
        if sub < 2:
            return
        allgather(stb_d[h], stg_d[h], [("stb", h)], [("stg", h)], f"st{par}", P)

    def head_rec(h):
        par = h % 2
        gst = cf(R_GST, 4 * SW).rearrange("p (r c) -> p r c", r=4)
        sp_dma(gst, stg_d[h].rearrange("(r p) c -> p r c", p=P), [("stg", h)], [("gst",)], "gst")
        S32 = [cf(R_S32 + d * P, P) for d in range(2)]
        Sbf = [cb(R_SBF + d * 64, P) for d in range(2)]
        attm = [cb(R_ATT + i * 64, P) for i in range(2)]
        tmp = cf(R_TMP, P)
        deff = cf(R_DEF, 8)
        sog = cb(A_SOG + par * 512, 1024)
        for d in range(2):
            order = [0, 1, 2] if d == 0 else [3, 2, 1]
            kS = ("S", d)
            for n, i in enumerate(order):
                m_i = posm[:, 4 * d + i:4 * d + i + 1]
                om_i = posm[:, 8 + 4 * d + i:8 + 4 * d + i + 1]
                Si = gst[:, i, d * P:(d + 1) * P]
                Di = gst[:, i, 2 * P + d:2 * P + d + 1]
                if n == 0:
                    ts(S32[d], Si, m_i, None, ALU.mult, r=[("gst",), "posm"], w=[kS])
                else:
                    stt(deff[:, d:d + 1], Di, m_i, om_i, ALU.mult, ALU.add, [("gst",), "posm"], [("deff", d)])
                    ts(tmp, Si, m_i, None, ALU.mult, r=[("gst",), "posm"], w=[("rtmp",)])
                    stt(S32[d], S32[d], deff[:, d:d + 1], tmp, ALU.mult, ALU.add, [kS, ("deff", d), ("rtmp",)], [kS])
            cp(Sbf[d], S32[d], [kS], [("Sbf", d)])
        oT = ps[:, B_O:B_O + 2, :]
        first = [True, True]
        ai = 0
        for i in range(NT):
            for d in range(2):
                qd, kd, ketm, dcs = pers(par, d)
                kP = ("pers", par, d)
                t_i = i if d == 0 else NT - 1 - i
                tsl = slice(t_i * P, (t_i + 1) * P)
                ob = B_O + t_i // 4
                ocol = (t_i % 4) * P
                mm(bank(B_ATT)[:, 0:P], kd[:, tsl], qd[:, tsl], True, True, r=[kP], w=[PK(B_ATT)])
                am = attm[ai % 2]
                kA = ("attm", ai % 2)
                ai += 1
                tt(am, bank(B_ATT)[:, 0:P], mask[d], ALU.mult, [PK(B_ATT), "cst"], [PK(B_ATT), kA])
                mm(bank(ob)[:, ocol:ocol + P], vtm[:, t_i, h * P:(h + 1) * P], am, first[t_i // 4], False,
                   r=[("vtm", t_i), kA], w=[PK(ob)])
                first[t_i // 4] = False
                chunks = [0, 1] if d == 0 else [1, 0]
                for cc in chunks:
                    n = 2 * t_i + cc
                    c0 = cc * CH
                    csl = slice(t_i * P + c0, t_i * P + c0 + CH)
                    mm(bank(ob)[:, ocol + c0:ocol + c0 + CH], Sbf[d], qd[:, csl], False, False,
                       r=[("Sbf", d), kP], w=[PK(ob)])
                    mm(bank(B_U)[:, 0:P], ketm[c0:c0 + CH, tsl], vtm[c0:c0 + CH, t_i, h * P:(h + 1) * P], True, True,
                       r=[kP, ("vtm", t_i)], w=[PK(B_U)])
                    stt(S32[d], S32[d], dcs[:, n:n + 1], bank(B_U)[:, 0:P], ALU.mult, ALU.add,
                        [("S", d), kP, PK(B_U)], [("S", d), PK(B_U)])
                    cp(Sbf[d], S32[d], [("S", d)], [("Sbf", d)], eng="act")
        sq = cf(R_SQ, 1024)
        rs = cf(R_RS, 1024)
        oTf = oT.rearrange("p b c -> p (b c)")
        if h == 0 and "oT0" in dbg_d:
            cp(sq, oTf, [PK(B_O), PK(B_O + 1)], [PK(B_O), PK(B_O + 1), ("sq",)])
            finals.append(sp_dma(dbg_d["oT0"], sq, [("sq",)], [("dbg", "oT0")], "dbg_oT0"))
        act(sq, oTf, AF.Square, [PK(B_O), PK(B_O + 1)], [PK(B_O), PK(B_O + 1), ("sq",)])
        for half in range(2):
            b = nb()
            hs = slice(half * 512, (half + 1) * 512)
            mm(bank(b), onesm, sq[:, hs], True, True, r=["cst", ("sq",)], w=[PK(b)])
            act(rs[:, hs], bank(b), AF.Ln, [PK(b), "epsc"], [PK(b), ("rs",)], bias=epsc, scale=1.0)
        act(rs, rs, AF.Exp, [("rs",)], [("rs",)], scale=-0.5)
        tt(sq, oTf, rs, ALU.mult, [PK(B_O), PK(B_O + 1), ("rs",)], [PK(B_O), PK(B_O + 1), ("sq",)])
        stt(hgT[:, h, :], sq, r2T[:, 32:33], sog, ALU.mult, ALU.mult, [("sq",), "r2T", ("sog", par)], [("hgT", h)])

    if sub == 0:
        return finish()
    if sub < 99:
        head_prep(0)
        if sub >= 3:
            head_rec(0)
        return finish()
    for h in range(H + 1):
        if h < H:
            head_prep(h)
        if h >= 1:
            head_rec(h - 1)
    if "hgT" in dbg_d:
        for h in range(H):
            dump_bf("hgT", hgT[:, h, :], [("hgT", h)], T, (h * P, (h + 1) * P), R_SQ + (h % 2) * 1024)
    if stage < 3:
        return finish()
    S.barrier()
    pool_banks[0] = list(range(8))

    uT = cb(0, H * 1024).rearrange("p (g c) -> p g c", g=H)
    vg = cf(8192, NT * 1024).rearrange("p (t c) -> p t c", t=NT)
    vn = cb(16384, NT * 1024).rearrange("p (t c) -> p t c", t=NT)
    sguT = cb(20480, H * 1024).rearrange("p (g c) -> p g c", g=H)
    junk3 = cb(24576, 1024)
    for cg in range(2):
        wi = load_w(win_d[:, C_ZU + cg * 512:C_ZU + (cg + 1) * 512])
        for hh in range(4):
            g = cg * 4 + hh
            for half in range(2):
                b = nb()
                proj_fm(wi, hh, half, b)
                act(uT[:, g, half * 512:(half + 1) * 512], bank(b), AF.Gelu, [PK(b)], [PK(b), ("uT", g)])
    for cg in range(2):
        wi = load_w(win_d[:, C_ZV + cg * 512:C_ZV + (cg + 1) * 512])
        for t_i in range(NT):
            b = nb()
            proj_tm(wi, t_i, b)
            act(vg[:, t_i, cg * 512:(cg + 1) * 512], bank(b), AF.Gelu, [PK(b)], [PK(b), ("vg", t_i), ("st", t_i)],
                accum=st[:, 4 * t_i + cg:4 * t_i + cg + 1])
    for t_i in range(NT):
        kS = ("st", t_i)
        c_s0 = st[:, 4 * t_i:4 * t_i + 1]
        c_s1 = st[:, 4 * t_i + 1:4 * t_i + 2]
        c_q = st[:, 4 * t_i + 2:4 * t_i + 3]
        c_m = st[:, 4 * t_i + 3:4 * t_i + 4]
        act(junk3, vg[:, t_i, :], AF.Square, [("vg", t_i)], [("junk3",), kS], accum=c_q)
        tt(c_m, c_s0, c_s1, ALU.add, [kS], [kS])
        ts(c_m, c_m, 1.0 / 1024, None, ALU.mult, r=[kS], w=[kS])
        ts(c_q, c_q, 1.0 / 1024, None, ALU.mult, r=[kS], w=[kS])
        stt(c_q, c_m, c_m, c_q, ALU.mult, ALU.subtract, [kS], [kS])
        act(c_q, c_q, AF.Ln, [kS, "epsc"], [kS], bias=epsc, scale=-1.0)
        act(c_q, c_q, AF.Exp, [kS], [kS], scale=-0.5)
        ts(c_m, c_m, c_q, None, ALU.mult, r=[kS], w=[kS])
        ts(c_m, c_m, -1.0, None, ALU.mult, r=[kS], w=[kS])
        act(vg[:, t_i, :], vg[:, t_i, :], AF.Identity, [("vg", t_i), kS], [("vg", t_i)], bias=c_m, scale=c_q)
        tt(vn[:, t_i, :], vg[:, t_i, :], gsg[:], ALU.mult, [("vg", t_i), "gsg"], [("vn", t_i)])
    for g in range(H):
        for half in range(2):
            b = nb()
            for q in range(4):
                t_i = half * 4 + q
                mm(bank(b)[:, q * P:(q + 1) * P], vn[:, t_i, g * P:(g + 1) * P], wsT[:, g, :], q == 0, False,
                   r=[("vn", t_i), "wsT"], w=[PK(b)])
                mm(bank(b)[:, q * P:(q + 1) * P], ones1[0:1, :], bsp[0:1, g * P:(g + 1) * P], False, q == 3,
                   r=["ones1", "bsp"], w=[PK(b)])
            hs = slice(half * 512, (half + 1) * 512)
            tt(sguT[:, g, hs], bank(b), uT[:, g, hs], ALU.mult, [PK(b), ("uT", g)], [PK(b), ("sguT", g)])
    if "sguT" in dbg_d:
        for g in range(H):
            dump_bf("sguT", sguT[:, g, :], [("sguT", g)], T, (g * P, (g + 1) * P), 8192 + (g % 2) * 1024)
    if stage < 4:
        return finish()
    S.barrier()

    mergedT = cb(8192, KC * 1024).rearrange("p (k c) -> p k c", k=KC)
    t12 = [cf(i * 512, 512) for i in range(4)]
    mi = 0
    for dg in range(4):
        wga = load_w(win_d[:, C_GA + dg * 512:C_GA + (dg + 1) * 512])
        wgb = load_w(win_d[:, C_GB + dg * 512:C_GB + (dg + 1) * 512])
        wab = wslot()
        pool_dma(wb[wab][:, 0:8, :], waout_d[:, dg * 512:(dg + 1) * 512].rearrange("(kc p) c -> p kc c", p=P), wab)
        pool_dma(wb[wab][:, 8:16, :], wbout_d[:, dg * 512:(dg + 1) * 512].rearrange("(kc p) c -> p kc c", p=P), wab)
        for hh in range(4):
            dch = dg * 4 + hh
            for half in range(2):
                hs = slice(half * 512, (half + 1) * 512)
                bga = nb()
                proj_fm(wga, hh, half, bga)
                bgb = nb()
                proj_fm(wgb, hh, half, bgb)
                bya = nb()
                for kc in range(8):
                    mm(bank(bya), wb[wab][:, kc, hh * P:(hh + 1) * P], hgT[:, kc, hs], kc == 0, kc == 7,
                       r=[("wb", wab), ("hgT", kc)], w=[PK(bya)])
                byb = nb()
                for kc in range(8):
                    mm(bank(byb), wb[wab][:, 8 + kc, hh * P:(hh + 1) * P], sguT[:, kc, hs], kc == 0, kc == 7,
                       r=[("wb", wab), ("sguT", kc)], w=[PK(byb)])
                t1 = t12[(mi % 2) * 2]
                t2 = t12[(mi % 2) * 2 + 1]
                k1 = ("t12", (mi % 2) * 2)
                k2 = ("t12", (mi % 2) * 2 + 1)
                mi += 1
                act(t1, bank(bga), AF.Sigmoid, [PK(bga)], [PK(bga), k1])
                act(t2, bank(bgb), AF.Sigmoid, [PK(bgb)], [PK(bgb), k2])
                tt(t1, bank(bya), t1, ALU.mult, [PK(bya), k1], [PK(bya), k1])
                tt(t2, bank(byb), t2, ALU.mult, [PK(byb), k2], [PK(byb), k2])
                tt(mergedT[:, dch, hs], t1, t2, ALU.add, [k1, k2], [("mg", dch, half)])
    if "mergedT" in dbg_d:
        for k in range(KC):
            dump_bf("mergedT", mergedT[:, k, :], [("mg", k, 0), ("mg", k, 1)], T, (k * P, (k + 1) * P), 16384 + (k % 2) * 1024)
    if stage < 5 and sub == 40:
        return finish()
    S.barrier()

    xts = [cf(i * D, D) for i in range(2)]
    gg1 = cf(2 * D, D)
    gtmp = cf(3 * D, D)
    yh = cf(16384, 4 * D).rearrange("p (q c) -> p q c", q=4)
    junk4 = cb(24576, D)
    modflat = modg_d.rearrange("r c -> (r c)")

    def load_gg(dst, tmpb, which, key):
        sp_dma(dst, modflat[(2 + 3 * which) * D:(3 + 3 * which) * D].partition_broadcast(P), ["modg"], [key], "gg")
        sp_dma(tmpb, gpost_d[which, :].partition_broadcast(P), [], [(key, "t")], "ggt")
        tt(dst, dst, tmpb, ALU.mult, [key, (key, "t")], [key])
    load_gg(gg1, gtmp, 0, ("gg", 0))

    def rstd_from4(q, kS):
        c = [st[:, 16 + 4 * q + i:16 + 4 * q + i + 1] for i in range(4)]
        tt(c[0], c[0], c[1], ALU.add, [kS], [kS])
        tt(c[2], c[2], c[3], ALU.add, [kS], [kS])
        tt(c[0], c[0], c[2], ALU.add, [kS], [kS])
        act(c[0], c[0], AF.Ln, [kS, "epsc"], [kS], bias=epsc, scale=1.0 / D)
        act(c[0], c[0], AF.Exp, [kS], [kS], scale=-0.5)
        return c[0]

    xcnt = [0]

    def wo_half(half):
        for cbk in range(4):
            wi = load_w(wo_d[:, cbk * 512:(cbk + 1) * 512])
            for q in range(4):
                t_i = half * 4 + q
                b = nb()
                for kc in range(KC):
                    mm(bank(b), mergedT[:, kc, t_i * P:(t_i + 1) * P], wb[wi][:, kc, :], kc == 0, kc == KC - 1,
                       r=[("wb", wi), ("mg", kc, half)], w=[PK(b)])
                cp(yh[:, q, cbk * 512:(cbk + 1) * 512], bank(b), [PK(b)], [PK(b), ("yh", q)])
                act(junk4[:, 0:512], bank(b), AF.Square, [PK(b)], [PK(b), ("junk4",), ("st4", q)],
                    accum=st[:, 16 + 4 * q + cbk:16 + 4 * q + cbk + 1])

    def get_h(t_i):
        half, q = t_i // 4, t_i % 4
        if q == 0:
            wo_half(half)
        kS = ("st4", q)
        rs = rstd_from4(q, kS)
        xi = xcnt[0] % 2
        xcnt[0] += 1
        sp_dma(xts[xi], x_d[t_i * P:(t_i + 1) * P, :], [], [("xts", xi)], f"xts{xi}")
        stt(yh[:, q, :], yh[:, q, :], rs, gg1, ALU.mult, ALU.mult, [("yh", q), kS, ("gg", 0)], [("yh", q)])
        tt(yh[:, q, :], yh[:, q, :], xts[xi], ALU.add, [("yh", q), ("xts", xi)], [("yh", q)])
        sp_dma(out_d[t_i * P:(t_i + 1) * P, :], yh[:, q, :], [("yh", q)], [("out", t_i)], "hsp")
        if "h" in dbg_d:
            finals.append(sp_dma(dbg_d["h"][t_i * P:(t_i + 1) * P, :], yh[:, q, :], [("yh", q)], [("dbgh", t_i)], "dbg_h"))
        return yh[:, q, :], [("yh", q)]

    norm_to_aT(get_h, 24576, 1)
    if "a2T" in dbg_d:
        for kc in range(KC):
            dump_bf("a2T", aT[:, kc, :], [("aT", kc, 0), ("aT", kc, 1)], T, (kc * P, (kc + 1) * P), (kc % 2) * T)
    if stage < 5:
        return finish()
    S.barrier()

    wcount[0] = 0
    NWc[0] = 2
    w2f = wb[2][:].rearrange("p k c -> p (k c)").bitcast(F32)
    gg2 = w2f[:, 0:D]
    hb = w2f[:, D:2 * D]
    hid = cb(0, 64 * 512).rearrange("p (k c) -> p k c", k=64)
    y2 = cf(16384, 4 * D).rearrange("p (q c) -> p q c", q=4)
    rts = [cf(24576 + i * 512, 512) for i in range(2)]
    junk5 = cb(25600, 512)
    load_gg(gg2, y2[:, 0, :], 1, ("gg", 1))
    ri = 0
    for half in range(2):
        hs = slice(half * 512, (half + 1) * 512)
        pool_banks[0] = list(range(8))
        for hg16 in range(16):
            wi = load_w(wff1_d[:, hg16 * 512:(hg16 + 1) * 512])
            for hh in range(4):
                hc = hg16 * 4 + hh
                b = nb()
                proj_fm(wi, hh, half, b)
                rt = rts[ri % 2]
                kR = ("rt", ri % 2)
                ri += 1
                act(rt, bank(b), AF.Relu, [PK(b)], [PK(b), kR])
                tt(hid[:, hc, :], rt, rt, ALU.mult, [kR], [("hid", hc)])
        for cbp in range(2):
            for kg in range(8):
                wi = wslot()
                w2v = wb[wi][:].rearrange("p (a b) c -> p a (b c)", a=8)
                pool_dma(w2v, wff2_d[kg * 1024:(kg + 1) * 1024, cbp * 1024:(cbp + 1) * 1024].rearrange("(hc p) c -> p hc c", p=P), wi)
                for hc8 in range(8):
                    hc = kg * 8 + hc8
                    for q in range(4):
                        for cb2 in range(2):
                            bnk = q * 2 + cb2
                            mm(bank(bnk), hid[:, hc, q * P:(q + 1) * P], w2v[:, hc8, cb2 * 512:(cb2 + 1) * 512],
                               hc == 0, hc == 63, r=[("hid", hc), ("wb", wi)], w=[PK(bnk)])
            for q in range(4):
                for cb2 in range(2):
                    bnk = q * 2 + cb2
                    col = cbp * 1024 + cb2 * 512
                    cp(y2[:, q, col:col + 512], bank(bnk), [PK(bnk)], [PK(bnk), ("y2", q)])
                    act(junk5, bank(bnk), AF.Square, [PK(bnk)], [PK(bnk), ("junk5",), ("st5", q)],
                        accum=st[:, 16 + 4 * q + cbp * 2 + cb2:16 + 4 * q + cbp * 2 + cb2 + 1])
        for q in range(4):
            t_i = half * 4 + q
            kS = ("st5", q)
            rs = rstd_from4(q, kS)
            sp_dma(hb, out_d[t_i * P:(t_i + 1) * P, :], [("out", t_i)], [("hb",)], "hb")
            stt(y2[:, q, :], y2[:, q, :], rs, gg2, ALU.mult, ALU.mult, [("y2", q), kS, ("gg", 1)], [("y2", q)])
            tt(y2[:, q, :], y2[:, q, :], hb, ALU.add, [("y2", q), ("hb",)], [("y2", q)])
            finals.append(sp_dma(out_d[t_i * P:(t_i + 1) * P, :], y2[:, q, :], [("y2", q), ("hb",)], [("out", t_i)], "ost"))
    return finish()


_CACHE = {}


def make_in_maps(inputs, stage=99):
    f = lambda a: np.ascontiguousarray(np.asarray(a, dtype=np.float32))
    x = f(inputs["x"]); c = f(inputs["c"])
    w_ada = f(inputs["w_ada"])[0]; b_ada = f(inputs["b_ada"])[0]
    lb = f(inputs["lb_logits"])
    cstm = np.zeros((P, 4 * P), np.float32)
    cstm[:, 0:P] = np.eye(P, dtype=np.float32)
    s_idx = np.arange(P)[:, None]
    c_idx = np.arange(P)[None, :]
    same = (s_idx // CH) == (c_idx // CH)
    cstm[:, P:2 * P] = (same & (s_idx <= c_idx)).astype(np.float32)
    cstm[:, 2 * P:3 * P] = (same & (s_idx >= c_idx)).astype(np.float32)
    cstm[:, 3 * P:4 * P] = 1.0 / P
    rows = np.zeros((P, P), np.float32)
    rows[96:112] = f(inputs["g_pre_mix"])[0].reshape(16, P)
    rows[112:128] = f(inputs["g_pre_ffn"])[0].reshape(16, P)
    rows2 = np.zeros((P, P), np.float32)
    rows2[0:32] = lb.reshape(32, P)
    rows2[32] = f(inputs["g_hgrn_norm"])[0]
    gpost = np.stack([f(inputs["g_post_mix"])[0], f(inputs["g_post_ffn"])[0]])
    shared = {
        "rows": rows, "rows2": rows2, "gpost": gpost,
        "gsgu": f(inputs["g_sgu_norm"]), "bsp": f(inputs["b_spatial"])[0].reshape(1, 1024),
        "wsp": f(inputs["w_spatial"])[0], "w_in": f(inputs["w_in"])[0],
        "w_a_out": f(inputs["w_a_out"])[0], "w_b_out": f(inputs["w_b_out"])[0], "w_o": f(inputs["w_o"])[0],
        "w_ff1": f(inputs["w_ff1"])[0], "w_ff2": f(inputs["w_ff2"])[0], "cst": cstm,
    }
    small = np.zeros((1, 1), np.float32)
    for name, st_min in (("w_in", 2), ("w_a_out", 4), ("w_b_out", 4), ("w_o", 4), ("w_ff1", 5), ("w_ff2", 5)):
        if stage < st_min:
            shared[name] = small
    maps = []
    for core in range(8):
        b, j = core // 4, core % 4
        m = dict(shared)
        m["x"] = np.ascontiguousarray(x[b, j * T:(j + 1) * T])
        m["cT"] = np.ascontiguousarray(c[b].reshape(KC, P).T)
        m["w_ada"] = np.ascontiguousarray(w_ada[:, j * 3072:(j + 1) * 3072])
        m["b_ada"] = np.ascontiguousarray(b_ada[j * 3072:(j + 1) * 3072].reshape(1, 3072))
        pm = np.zeros((P, 8), np.float32)
        for i in range(4):
            pm[:, i] = 1.0 if i < j else 0.0
            pm[:, 4 + i] = 1.0 if i > j else 0.0
        m["posm"] = pm
        maps.append(m)
    return maps


def kernel(**inputs):
    if "nc" not in _CACHE:
        _CACHE["nc"] = build()
    nc, _stack = _CACHE["nc"]
    maps = make_in_maps(inputs)
    res = run_bass_kernel_spmd(nc, maps, core_ids=list(range(8)))
    out = np.zeros((2, 4096, D), np.float32)
    for core in range(8):
        b, j = core // 4, core % 4
        out[b, j * T:(j + 1) * T] = res.results[core]["out"]
    return out
```

```python
from contextlib import ExitStack
import numpy as np
import concourse.bass as bass
import concourse.mybir as mybir
from concourse.bass_utils import run_bass_kernel_spmd

F32 = mybir.dt.float32
BF16 = mybir.dt.bfloat16
AF = mybir.ActivationFunctionType
ALU = mybir.AluOpType

P = 128
T = 1024
D = 2048
KC = 16
NT = 8
H = 8
CH = 64
NCH = T // CH
DFF = 8192
EPS = 1e-6
INW = 11264
C_Q, C_FF, C_FB, C_V, C_OG, C_ZU, C_ZV, C_GA, C_GB = 0, 1024, 2048, 3072, 4096, 5120, 6144, 7168, 9216
SW = 2 * P + 2
GROUPS8 = [[0, 1, 2, 3], [4, 5, 6, 7]]
STD_TILES = [6, 7] + list(range(10, 22))


class Sched:
    ENGS = ("pe", "act", "dve", "pool", "sp")

    def __init__(self):
        self.ops = {e: [] for e in self.ENGS}
        self.last_w = {}
        self.readers = {}
        self.dma_cnt = {}
        self.known = {e: {} for e in self.ENGS}
        self.signal = {e: set() for e in self.ENGS}

    def _need(self, eng, tok, needs, war=False):
        if tok is None:
            return
        if tok[0] == eng and (eng == "pe" or war):
            return
        needs.append(tok)

    def _waits(self, eng, needs):
        best = {}
        for kind, idx in needs:
            if idx > best.get(kind, -1):
                best[kind] = idx
        waits = []
        for kind, idx in best.items():
            if self.known[eng].get(kind, -1) >= idx:
                continue
            self.known[eng][kind] = idx
            waits.append((kind, idx))
            if kind in self.ENGS:
                self.signal[kind].add(idx)
        return waits

    def op(self, eng, fn, r=(), w=(), dma=None):
        needs = []
        for k in r:
            self._need(eng, self.last_w.get(k), needs)
        for k in w:
            self._need(eng, self.last_w.get(k), needs, war=True)
            for t in self.readers.get(k, ()):
                self._need(eng, t, needs, war=True)
        waits = self._waits(eng, needs)
        if dma is None:
            tok = (eng, len(self.ops[eng]))
        else:
            n = self.dma_cnt.get(dma, 0)
            self.dma_cnt[dma] = n + 1
            tok = ("dma:" + dma, n)
        self.ops[eng].append((fn, waits, dma))
        for k in r:
            self.readers.setdefault(k, []).append(tok)
        for k in w:
            self.last_w[k] = tok
            self.readers[k] = []
        return tok

    def barrier(self):
        toks = []
        for e in self.ENGS:
            for i in range(len(self.ops[e]) - 1, -1, -1):
                if self.ops[e][i][0] is not None and self.ops[e][i][2] is None:
                    toks.append((e, i))
                    break
        for k, n in self.dma_cnt.items():
            toks.append(("dma:" + k, n - 1))
        for e in self.ENGS:
            needs = [t for t in toks if t[0] != e or e in ("act", "dve", "pool")]
            self.ops[e].append((None, self._waits(e, needs), None))

    def emit(self, nc, block, stack, final_tokens):
        sems = {}
        for e in self.ENGS:
            sems[e] = stack.enter_context(nc.semaphore("sem_" + e))
        for k in self.dma_cnt:
            sems["dma:" + k] = stack.enter_context(nc.semaphore("sd_" + k))
        rank = {}
        for e in self.ENGS:
            rank[e] = {idx: i + 1 for i, idx in enumerate(sorted(self.signal[e]))}
            assert len(rank[e]) < 60000, (e, len(rank[e]))

        def val(kind, idx):
            if kind in self.ENGS:
                return rank[kind][idx]
            if kind.startswith("dma:cc"):
                return idx + 1
            return 16 * (idx + 1)

        def run(ename, eobj, extra_final):
            for i, (fn, waits, dma) in enumerate(self.ops[ename]):
                for kind, idx in waits:
                    eobj.wait_ge(sems[kind], val(kind, idx))
                if fn is None:
                    continue
                ins = fn(eobj)
                if dma is not None:
                    if dma.startswith("cc"):
                        ins.then_inc(sems["dma:" + dma])
                    else:
                        ins.then_inc(sems["dma:" + dma], 16)
                elif i in self.signal[ename]:
                    ins.then_inc(sems[ename], 1)
            if extra_final:
                for kind, idx in final_tokens:
                    eobj.wait_ge(sems[kind], val(kind, idx))

        @block.tensor
        def _(e):
            run("pe", e, False)

        @block.scalar
        def _(e):
            run("act", e, False)

        @block.vector
        def _(e):
            run("dve", e, False)

        @block.gpsimd
        def _(e):
            run("pool", e, False)

        @block.sync
        def _(e):
            run("sp", e, True)


def build(stage=99, dbg=(), ncores=8, nocc=False, sub=99):
    nc = bass.Bass("TRN2", target_bir_lowering=False)
    stack = ExitStack()
    S = Sched()
    GROUPS = GROUPS8 if ncores == 8 else [[0, 1, 2, 3]]

    def din(name, shape, dt=F32):
        return nc.dram_tensor(name, list(shape), dt, kind="ExternalInput").ap()

    x_d = din("x", [T, D])
    cT_d = din("cT", [P, KC])
    TW = KC * 512
    wada_d = din("w_ada", [6, P, TW])
    bada_d = din("b_ada", [1, 3072])
    rows_d = din("rows", [P, P])
    rows2_d = din("rows2", [P, P])
    gpost_d = din("gpost", [2, D])
    gsgu_d = din("gsgu", [1, 1024])
    bsp_d = din("bsp", [1, 1024])
    wsp_d = din("wsp", [H, P, P])
    winstd_d = din("w_in_std", [len(STD_TILES), P, TW] if stage >= 2 else [1, 1, 1])
    winh_d = din("w_in_head", [H, P, TW] if stage >= 2 else [1, 1, 1])
    wab_d = din("w_ab_out", [4, P, TW] if stage >= 4 else [1, 1, 1])
    wo_d = din("w_o", [4, P, TW] if stage >= 4 else [1, 1, 1])
    wff1_d = din("w_ff1", [16, P, TW] if stage >= 5 else [1, 1, 1])
    wff2_d = din("w_ff2", [16, P, TW] if stage >= 5 else [1, 1, 1])
    cst_d = din("cst", [P, 4 * P])
    posm_d = din("posm", [P, 8])
    out_d = nc.dram_tensor("out", [T, D], F32, kind="ExternalOutput").ap()
    dbg_d = {}
    for name, shape in dbg:
        dbg_d[name] = nc.dram_tensor("dbg_" + name, list(shape), F32, kind="ExternalOutput").ap()

    modb_d = nc.dram_tensor("mod_bounce", [1, 3072], F32).ap()
    modg_d = nc.dram_tensor("mod_gath", [4, 3072], F32).ap()
    stb_d = [nc.dram_tensor(f"st_bounce{h}", [P, SW], F32).ap() for h in range(H)]
    stg_d = [nc.dram_tensor(f"st_gath{h}", [4 * P, SW], F32).ap() for h in range(H)]

    def sb(name, shape, dt=F32):
        return stack.enter_context(nc.sbuf_tensor("s_" + name, list(shape), dt))

    ps = stack.enter_context(nc.psum_tensor("ps", [P, 8, 512], F32))

    def bank(b):
        return ps[:, b, :]

    def bankbf(b):
        return ps[:, b, :].bitcast(BF16)

    def PK(b):
        return ("ps", b)

    cst = sb("cst", [P, 4 * P])
    ident = cst[:, 0:P]
    mask = [cst[:, P:2 * P], cst[:, 2 * P:3 * P]]
    onesm = cst[:, 3 * P:4 * P]
    identb = sb("identb", [P, P], BF16)
    ones1 = sb("ones1", [1, P])
    posm = sb("posm", [P, 16])
    modT = sb("modT", [P, P])
    gm = sb("gm", [P, 32])
    r2T = sb("r2T", [P, P])
    lbv = sb("lbv", [P, 64])
    gsg = sb("gsg", [P, 1024])
    bsp = sb("bsp", [1, 1024])
    wsT = sb("wsT", [P, H, P], BF16)
    zeros = sb("zeros", [P, 1024])
    aT = sb("aT", [P, KC, T], BF16)
    NW = 3
    wb = [sb(f"wb{i}", [P, KC, 512], BF16) for i in range(NW)]
    st = sb("stat", [P, 64])
    ARENA_W = 26500
    arena = sb("arena", [P, ARENA_W])

    def cf(off, n):
        return arena[:, off:off + n]

    def cb(off_words, n):
        return arena[:, off_words:off_words + n // 2].bitcast(BF16)

    wcount = [0]
    NWc = [NW]

    def wslot():
        i = wcount[0] % NWc[0]
        wcount[0] += 1
        return i

    def pool_dma(dst, src, i):
        S.op("pool", lambda e: e.dma_start(out=dst, in_=src), r=[], w=[("wb", i)], dma=f"wb{i}")

    def load_t(tile_ap):
        i = wslot()
        pool_dma(wb[i][:].rearrange("p k c -> p (k c)"), tile_ap, i)
        return i

    def load_in(col0):
        return load_t(winstd_d[STD_TILES.index(col0 // 512)])

    def sp_dma(out, in_, r, w, key):
        return S.op("sp", lambda e: e.dma_start(out=out, in_=in_), r=r, w=w, dma=key)

    def mm(out, lhsT, rhs, start, stop, r, w):
        S.op("pe", lambda e: e.matmul(out, lhsT, rhs, start=start, stop=stop, skip_group_check=True), r=r, w=w)

    def tr(out, in_, idn, r, w):
        S.op("pe", lambda e: e.transpose(out, in_, idn), r=r, w=w)

    def act(out, in_, func, r, w, bias=None, scale=None, accum=None):
        kw = {}
        if bias is not None:
            kw["bias"] = bias
        if scale is not None:
            kw["scale"] = scale
        if accum is not None:
            kw["accum_out"] = accum
        S.op("act", lambda e: e.activation(out, in_, func, **kw), r=r, w=w)

    def ts(out, in0, s1, s2, op0, op1=None, r=(), w=(), eng="dve"):
        if op1 is None:
            S.op(eng, lambda e: e.tensor_scalar(out, in0, s1, None, op0), r=r, w=w)
        else:
            S.op(eng, lambda e: e.tensor_scalar(out, in0, s1, s2, op0, op1), r=r, w=w)

    def tt(out, in0, in1, op, r, w, eng="dve"):
        S.op(eng, lambda e: e.tensor_tensor(out, in0, in1, op), r=r, w=w)

    def stt(out, in0, scalar, in1, op0, op1, r, w):
        S.op("dve", lambda e: e.scalar_tensor_tensor(out, in0, scalar, in1, op0, op1), r=r, w=w)

    def cp(out, in_, r, w, eng="dve"):
        if eng == "act":
            S.op("act", lambda e: e.copy(out, in_), r=r, w=w)
        else:
            S.op(eng, lambda e: e.tensor_copy(out, in_), r=r, w=w)

    pb = [0]
    pool_banks = [list(range(8))]

    def nb():
        lst = pool_banks[0]
        b = lst[pb[0] % len(lst)]
        pb[0] += 1
        return b

    finals = []

    def dump(name, src_ap, rkeys, rows=None):
        if name in dbg_d:
            dst = dbg_d[name] if rows is None else dbg_d[name][rows[0]:rows[1], :]
            finals.append(sp_dma(dst, src_ap, r=rkeys, w=[("dbg", name, rows)], key="dbg_" + name))

    def dump_bf(name, src_bf_ap, rkeys, n, rows, scratch_off):
        if name in dbg_d:
            tmp = cf(scratch_off, n)
            cp(tmp, src_bf_ap, rkeys, [("dbgtmp", scratch_off)])
            finals.append(sp_dma(dbg_d[name][rows[0]:rows[1], :], tmp, [("dbgtmp", scratch_off)],
                                 [("dbg", name, rows)], "dbg_" + name))

    def finish():
        with nc.Block() as block:
            S.emit(nc, block, stack, finals)
        return nc, stack

    sp_dma(cst[:], cst_d, [], ["cst"], "c0")
    sp_dma(posm[:, 0:8], posm_d, [], ["posm"], "c1")
    SB0 = 12288
    rows = cf(SB0, P)
    rows2 = cf(SB0 + P, P)
    cTs = cf(SB0 + 2 * P, KC)
    sp_dma(cTs, cT_d, [], ["cTs"], "c4")
    sp_dma(cf(SB0 + 3 * P, 3072)[0:1, :], bada_d, [], ["badas"], "c7")
    for q in range(4):
        sp_dma(cf(q * D, D), x_d[q * P:(q + 1) * P, :], [], [("xt", q)], f"xt{q}")
    sp_dma(rows[96:128, :], rows_d[96:128, :], [], ["rows_hi"], "c2")
    sp_dma(rows2, rows2_d, [], ["rows2"], "c3")
    sp_dma(bsp[:], bsp_d, [], ["bsp"], "c5")
    badas = cf(SB0 + 3 * P, 3072)
    S.op("dve", lambda e: e.memset(ones1[:], 1.0), r=[], w=["ones1"])
    S.op("dve", lambda e: e.memset(zeros[:], 0.0), r=[], w=["zeros"])
    epsc = st[:, 63:64]
    S.op("dve", lambda e: e.memset(epsc, EPS), r=[], w=["epsc"])
    cp(identb[:], ident, ["cst"], ["identb"])
    ts(posm[:, 8:16], posm[:, 0:8], -1.0, 1.0, ALU.mult, ALU.add, r=["posm"], w=["posm"])

    csig = cf(SB0 + 3 * P + 3072, KC)
    act(csig, cTs, AF.Sigmoid, ["cTs"], ["csig"])
    csb = cb(SB0 + 3 * P + 3072 + KC, 2 * KC)[:, 0:KC]
    tt(csb, csig, cTs, ALU.mult, ["csig", "cTs"], ["csb"])
    modp = cf(SB0 + 3 * P + 3072 + 2 * KC, 3072)
    for j in range(6):
        wi = load_t(wada_d[j])
        b = nb()
        for kc in range(KC):
            mm(bank(b)[0:1, :], csb[:, kc:kc + 1], wb[wi][:, kc, :], kc == 0, kc == KC - 1,
               r=["csb", ("wb", wi)], w=[PK(b)])
        tt(modp[0:1, j * 512:(j + 1) * 512], bank(b)[0:1, :], badas[0:1, j * 512:(j + 1) * 512], ALU.add,
           [PK(b), "badas"], [PK(b), "modp"])
    sp_dma(modb_d, modp[0:1, :], ["modp"], ["modb"], "c8")
    def allgather(src_d, dst_d, rkeys, wkeys, key, nrows):
        if nocc:
            S.op("pool", lambda e: e.dma_start(out=dst_d[0:nrows, :], in_=src_d), r=rkeys, w=wkeys, dma="nocc_" + key)
        else:
            S.op("pool", lambda e: e.collective_compute("AllGather", ALU.bypass, replica_groups=GROUPS,
                                                        ins=[src_d.opt()], outs=[dst_d.opt()]),
                 r=rkeys, w=wkeys, dma="cc_" + key)
    allgather(modb_d, modg_d, ["modb"], ["modg"], "mod", 1)
    sp_dma(rows[0:96, :], modg_d.rearrange("r (j c) -> (r j) c", c=P), ["modg"], ["rows_lo"], "c9")
    b = nb()
    tr(bank(b)[:, 0:P], rows, ident, ["rows_lo", "rows_hi", "cst"], [PK(b)])
    cp(modT[:], bank(b)[:, 0:P], [PK(b)], [PK(b), "modT"])
    stt(gm[:, 0:16], modT[:, 16:32], 1.0, modT[:, 96:112], ALU.add, ALU.mult, ["modT"], ["gm"])
    stt(gm[:, 16:32], modT[:, 64:80], 1.0, modT[:, 112:128], ALU.add, ALU.mult, ["modT"], ["gm"])
    b = nb()
    tr(bank(b)[:, 0:P], rows2, ident, ["rows2", "cst"], [PK(b)])
    cp(r2T[:], bank(b)[:, 0:P], [PK(b)], [PK(b), "r2T"])
    for d in range(2):
        tt(lbv[:, 16 * d:16 * d + 8], r2T[:, 16 * d:16 * d + 8], r2T[:, 16 * d + 8:16 * d + 16], ALU.subtract,
           ["r2T"], ["lbv"])
        act(lbv[:, 16 * d:16 * d + 8], lbv[:, 16 * d:16 * d + 8], AF.Sigmoid, ["lbv"], ["lbv"])
        ts(lbv[:, 16 * d + 8:16 * d + 16], lbv[:, 16 * d:16 * d + 8], -1.0, 1.0, ALU.mult, ALU.add, r=["lbv"], w=["lbv"])
        ts(lbv[:, 32 + 16 * d:32 + 16 * d + 8], lbv[:, 16 * d + 8:16 * d + 16], -1.0, None, ALU.mult, r=["lbv"], w=["lbv"])
    wsl = cf(20480, H * P)
    sp_dma(wsl.rearrange("p (g s) -> p g s", g=H), wsp_d.rearrange("g t s -> t g s"), [], ["wsl"], "c14")
    for g in range(H):
        b = nb()
        tr(bank(b)[:, 0:P], wsl[:, g * P:(g + 1) * P], ident, ["wsl", "cst"], [PK(b)])
        cp(wsT[:, g, :], bank(b)[:, 0:P], [PK(b)], [PK(b), "wsT"])
    dump("modT", modT[:], ["modT"])
    dump("lbv", lbv[:], ["lbv"])

    def norm_to_aT(get_tile, junk_off, which):
        gmc = gm[:, 16 * which:16 * which + 16]
        shc = modT[:, 48 * which:48 * which + 16]
        for g4 in range(2):
            tiles = []
            for q in range(4):
                t_i = g4 * 4 + q
                xt, keys = get_tile(t_i)
                sq = cb(junk_off, D)
                sc = st[:, 2 * t_i:2 * t_i + 1]
                act(sq, xt, AF.Square, keys, [("junk",), ("st", t_i)], accum=sc)
                rs = st[:, 2 * t_i + 1:2 * t_i + 2]
                act(rs, sc, AF.Ln, [("st", t_i)], [("st", t_i)], bias=epsc, scale=1.0 / D)
                act(rs, rs, AF.Exp, [("st", t_i)], [("st", t_i)], scale=-0.5)
                ts(xt, xt, rs, None, ALU.mult, r=keys + [("st", t_i)], w=keys)
                tiles.append((xt, keys))
            for kc in range(KC):
                b = nb()
                for q in range(4):
                    xt, keys = tiles[q]
                    tr(bank(b)[:, q * P:(q + 1) * P], xt[:, kc * P:(kc + 1) * P], ident, keys + ["cst"], [PK(b)])
                act(aT[:, kc, g4 * 512:(g4 + 1) * 512], bank(b), AF.Identity, [PK(b), "gm", "modT"],
                    [PK(b), ("aT", kc, g4)], bias=shc[:, kc:kc + 1], scale=gmc[:, kc:kc + 1])

    def get_x(t_i):
        q = t_i % 4
        xt = cf(q * D, D)
        key = ("xt", q)
        if t_i >= 4:
            sp_dma(xt, x_d[t_i * P:(t_i + 1) * P, :], [], [key], f"xt{q}")
        return xt, [key]
    norm_to_aT(get_x, 4 * D, 0)
    if "aT" in dbg_d:
        for kc in range(KC):
            dump_bf("aT", aT[:, kc, :], [("aT", kc, 0), ("aT", kc, 1)], T, (kc * P, (kc + 1) * P), 4 * D + 1024 + (kc % 2) * T)
    if stage < 2:
        return finish()
    S.barrier()

    aT_all = [("aT", kc, g4) for kc in range(KC) for g4 in range(2)]

    def proj_fm(wi, hh, half, b):
        for kc in range(KC):
            mm(bank(b), wb[wi][:, kc, hh * P:(hh + 1) * P], aT[:, kc, half * 512:(half + 1) * 512],
               kc == 0, kc == KC - 1, r=[("wb", wi), ("aT", kc, half)], w=[PK(b)])

    def proj_tm(wi, t_i, b):
        for kc in range(KC):
            mm(bank(b), aT[:, kc, t_i * P:(t_i + 1) * P], wb[wi][:, kc, :],
               kc == 0, kc == KC - 1, r=[("wb", wi), ("aT", kc, t_i // 4)], w=[PK(b)])

    A_VTM = 0
    A_HG = 4096
    A_QT = 8192
    A_SOG = A_QT + 1024
    A_WK = A_SOG + 1024
    WKSZ = 4 * 1024
    A_PERS = A_WK + WKSZ
    PSZ = 3 * 512 + 16
    A_KG = A_PERS + 4 * PSZ
    A_STL = A_KG + 3 * 512
    A_REC = A_STL + 264
    R_GST = A_REC
    R_S32 = R_GST + 4 * SW
    R_SBF = R_S32 + 2 * P
    R_ATT = R_SBF + 2 * 64
    R_TMP = R_ATT + 2 * 64
    R_DEF = R_TMP + P
    R_SQ = R_DEF + 8
    R_RS = R_SQ + 1024
    A_END = R_RS + 1024
    assert A_END <= ARENA_W, A_END
    vtm = cb(A_VTM, NT * 1024).rearrange("p (t c) -> p t c", t=NT)
    hgT = cb(A_HG, H * 1024).rearrange("p (h c) -> p h c", h=H)
    pool_banks[0] = [0, 1, 2, 3]
    B_ATT, B_U, B_O = 4, 5, 6
    NWc[0] = 2
    wk1 = wb[2][:].rearrange("p k c -> p (k c)").bitcast(F32)
    gsg_bf = gsg[:].bitcast(BF16)

    def wkbuf(d, i):
        return cf(A_WK + i * 1024, 1024) if d == 0 else wk1[:, i * 1024:(i + 1) * 1024]

    for cg in range(2):
        wi = load_in(C_V + cg * 512)
        for t_i in range(NT):
            b = nb()
            proj_tm(wi, t_i, b)
            cp(vtm[:, t_i, cg * 512:(cg + 1) * 512], bank(b), [PK(b)], [PK(b), ("vtm", t_i)],
               eng="act" if t_i % 2 else "dve")

    def pers(par, d):
        base = A_PERS + (par * 2 + d) * PSZ
        return cb(base, 1024), cb(base + 512, 1024), cb(base + 1024, 1024), cf(base + 1536, 16)

    hw = {}

    def prep_common(h):
        par = h % 2
        wi = load_t(winh_d[h])
        hw[h] = wi
        qT = cf(A_QT, 1024)
        sog = cb(A_SOG + par * 512, 1024)
        for half in range(2):
            b = nb()
            proj_fm(wi, 0, half, b)
            S.op("act", lambda e, b=b, half=half: e.mul(qT[:, half * 512:(half + 1) * 512], bank(b), float(P) ** -0.5),
                 r=[PK(b)], w=[PK(b), ("qT",)])
        sgt = cf(R_SQ, 1024)
        for half in range(2):
            b = nb()
            proj_fm(wi, 3, half, b)
            hs = slice(half * 512, (half + 1) * 512)
            act(sgt[:, hs], bank(b), AF.Sigmoid, [PK(b)], [PK(b), ("sq",)])
            tt(sog[:, hs], bank(b), sgt[:, hs], ALU.mult, [PK(b), ("sq",)], [PK(b), ("sog", par)])

    def prep_dir(h, d):
        par = h % 2
        wi = hw[h]
        stl = cf(A_STL, SW)
        sig, lfb, Bf, E1 = (wkbuf(d, i) for i in range(4))
        kW = ("wk", d)
        qd, kd, ketm, dcs = pers(par, d)
        kP = ("pers", par, d)
        lb_c = lbv[:, 16 * d + h:16 * d + h + 1]
        oml_c = lbv[:, 16 * d + 8 + h:16 * d + 8 + h + 1]
        noml_c = lbv[:, 32 + 16 * d + h:32 + 16 * d + h + 1]
        kefm = cb(A_KG, 1024) if d == 0 else gsg_bf[:, 0:1024]
        kgfm = cb(A_KG + 512, 1024) if d == 0 else gsg_bf[:, 1024:2048]
        kgtm = cb(A_KG + 1024, 1024)
        kE, kG = ("kefm", d), ("kgfm", d)
        for half in range(2):
            b = nb()
            proj_fm(wi, 1 + d, half, b)
            act(sig[:, half * 512:(half + 1) * 512], bank(b), AF.Sigmoid, [PK(b)], [PK(b), kW])
        yield
        act(lfb, sig, AF.Ln, [kW, "lbv"], [kW], bias=lb_c, scale=oml_c)
        ts(sig, sig, noml_c, oml_c, ALU.mult, ALU.add, r=[kW, "lbv"], w=[kW])
        yield
        S.op("dve", lambda e: e.tensor_tensor_scan(Bf, lfb, zeros[:], 0.0, ALU.add, ALU.add), r=[kW, "zeros"], w=[kW])
        yield
        B3 = Bf.rearrange("p (n c) -> p n c", c=CH)
        lf3 = lfb.rearrange("p (n c) -> p n c", c=CH)
        if d == 0:
            Bs = B3[:, 0:NCH - 1, CH - 1:CH].to_broadcast([P, NCH - 1, CH])
            tt(lf3[:, 1:NCH, :], B3[:, 1:NCH, :], Bs, ALU.subtract, [kW], [kW])
            cp(lf3[:, 0, :], B3[:, 0, :], [kW], [kW])
        else:
            Be = B3[:, :, CH - 1:CH].to_broadcast([P, NCH, CH])
            tt(lf3, lf3, Be, ALU.add, [kW], [kW])
            tt(lfb, lfb, Bf, ALU.subtract, [kW], [kW])
        yield
        act(E1, lfb, AF.Exp, [kW], [kW])
        act(lfb, lfb, AF.Exp, [kW], [kW], scale=-1.0)
        yield
        tt(qd, cf(A_QT, 1024), E1, ALU.mult, [kW, ("qT",)], [kP])
        tt(lfb, sig, lfb, ALU.mult, [kW], [kW])
        yield
        cp(kd, lfb, [kW], [kP])
        E13 = E1.rearrange("p (n c) -> p n c", c=CH)
        last = CH - 1 if d == 0 else 0
        cp(dcs, E13[:, :, last], [kW], [kP])
        tt(kefm.rearrange("p (n c) -> p n c", c=CH), lf3, E13[:, :, last:last + 1].to_broadcast([P, NCH, CH]),
           ALU.mult, [kW], [kE])
        yield
        act(stl[:, 2 * P + d:2 * P + d + 1], Bf[:, T - 1:T], AF.Exp, [kW], [("stl",)])
        if d == 0:
            act(E1, Bf, AF.Exp, [kW], [kW], bias=Bf[:, T - 1:T], scale=-1.0)
        else:
            act(E1[:, 1:T], Bf[:, 0:T - 1], AF.Exp, [kW], [kW])
            S.op("dve", lambda e: e.memset(E1[:, 0:1], 1.0), r=[], w=[kW])
        yield
        tt(kgfm, sig, E1, ALU.mult, [kW], [kG])
        b1 = nb()
        for t_i in range(NT):
            tr(bankbf(b1)[:, t_i * P:(t_i + 1) * P], kefm[:, t_i * P:(t_i + 1) * P], identb[:], [kE, "identb"], [PK(b1)])
        cp(ketm, bankbf(b1), [PK(b1)], [PK(b1), kP], eng="act")
        yield
        b2 = nb()
        for t_i in range(NT):
            tr(bankbf(b2)[:, t_i * P:(t_i + 1) * P], kgfm[:, t_i * P:(t_i + 1) * P], identb[:], [kG, "identb"], [PK(b2)])
        cp(kgtm, bankbf(b2), [PK(b2)], [PK(b2), ("kgtm",)])
        b3 = nb()
        for t_i in range(NT):
            mm(bank(b3)[:, 0:P], kgtm[:, t_i * P:(t_i + 1) * P], vtm[:, t_i, h * P:(h + 1) * P], t_i == 0, t_i == NT - 1,
               r=[("kgtm",), ("vtm", t_i)], w=[PK(b3)])
        cp(stl[:, d * P:(d + 1) * P], bank(b3)[:, 0:P], [PK(b3)], [PK(b3), ("stl",)])

    def prep_finish(h):
        par = h % 2
        stl = cf(A_STL, SW)
        sp_dma(stb_d[h], stl, [("stl",)], [("stb", h)], f"stb{par}")
        allgather(stb_d[h], stg_d[h], [("stb", h)], [("stg", h)], f"st{par}", P)

    def head_rec(h):
        par = h % 2
        gst = cf(R_GST, 4 * SW).rearrange("p (r c) -> p r c", r=4)
        sp_dma(gst, stg_d[h].rearrange("(r p) c -> p r c", p=P), [("stg", h)], [("gst",)], "gst")
        S32 = [cf(R_S32 + d * P, P) for d in range(2)]
        Sbf = [cb(R_SBF + d * 64, P) for d in range(2)]
        attm = [cb(R_ATT + i * 64, P) for i in range(2)]
        tmp = cf(R_TMP, P)
        deff = cf(R_DEF, 8)
        sog = cb(A_SOG + par * 512, 1024)
        for d in range(2):
            order = [0, 1, 2] if d == 0 else [3, 2, 1]
            kS = ("S", d)
            for n, i in enumerate(order):
                m_i = posm[:, 4 * d + i:4 * d + i + 1]
                om_i = posm[:, 8 + 4 * d + i:8 + 4 * d + i + 1]
                Si = gst[:, i, d * P:(d + 1) * P]
                Di = gst[:, i, 2 * P + d:2 * P + d + 1]
                if n == 0:
                    ts(S32[d], Si, m_i, None, ALU.mult, r=[("gst",), "posm"], w=[kS])
                else:
                    stt(deff[:, d:d + 1], Di, m_i, om_i, ALU.mult, ALU.add, [("gst",), "posm"], [("deff", d)])
                    ts(tmp, Si, m_i, None, ALU.mult, r=[("gst",), "posm"], w=[("rtmp",)])
                    stt(S32[d], S32[d], deff[:, d:d + 1], tmp, ALU.mult, ALU.add, [kS, ("deff", d), ("rtmp",)], [kS])
            cp(Sbf[d], S32[d], [kS], [("Sbf", d)])
            yield
        oT = ps[:, B_O:B_O + 2, :]
        first = [True, True]
        ai = 0
        for i in range(NT):
            for d in range(2):
                qd, kd, ketm, dcs = pers(par, d)
                kP = ("pers", par, d)
                t_i = i if d == 0 else NT - 1 - i
                tsl = slice(t_i * P, (t_i + 1) * P)
                ob = B_O + t_i // 4
                ocol = (t_i % 4) * P
                mm(bank(B_ATT)[:, 0:P], kd[:, tsl], qd[:, tsl], True, True, r=[kP], w=[PK(B_ATT)])
                am = attm[ai % 2]
                kA = ("attm", ai % 2)
                ai += 1
                tt(am, bank(B_ATT)[:, 0:P], mask[d], ALU.mult, [PK(B_ATT), "cst"], [PK(B_ATT), kA])
                mm(bank(ob)[:, ocol:ocol + P], vtm[:, t_i, h * P:(h + 1) * P], am, first[t_i // 4], False,
                   r=[("vtm", t_i), kA], w=[PK(ob)])
                first[t_i // 4] = False
                chunks = [0, 1] if d == 0 else [1, 0]
                for cc in chunks:
                    n = 2 * t_i + cc
                    c0 = cc * CH
                    csl = slice(t_i * P + c0, t_i * P + c0 + CH)
                    mm(bank(ob)[:, ocol + c0:ocol + c0 + CH], Sbf[d], qd[:, csl], False, False,
                       r=[("Sbf", d), kP], w=[PK(ob)])
                    mm(bank(B_U)[:, 0:P], ketm[c0:c0 + CH, tsl], vtm[c0:c0 + CH, t_i, h * P:(h + 1) * P], True, True,
                       r=[kP, ("vtm", t_i)], w=[PK(B_U)])
                    stt(S32[d], S32[d], dcs[:, n:n + 1], bank(B_U)[:, 0:P], ALU.mult, ALU.add,
                        [("S", d), kP, PK(B_U)], [("S", d), PK(B_U)])
                    cp(Sbf[d], S32[d], [("S", d)], [("Sbf", d)], eng="act")
                yield
        sq = cf(R_SQ, 1024)
        rs = cf(R_RS, 1024)
        oTf = oT.rearrange("p b c -> p (b c)")
        if h == 0 and "oT0" in dbg_d:
            cp(sq, oTf, [PK(B_O), PK(B_O + 1)], [PK(B_O), PK(B_O + 1), ("sq",)])
            finals.append(sp_dma(dbg_d["oT0"], sq, [("sq",)], [("dbg", "oT0")], "dbg_oT0"))
        act(sq, oTf, AF.Square, [PK(B_O), PK(B_O + 1)], [PK(B_O), PK(B_O + 1), ("sq",)])
        for half in range(2):
            b = nb()
            hs = slice(half * 512, (half + 1) * 512)
            mm(bank(b), onesm, sq[:, hs], True, True, r=["cst", ("sq",)], w=[PK(b)])
            act(rs[:, hs], bank(b), AF.Ln, [PK(b), "epsc"], [PK(b), ("rs",)], bias=epsc, scale=1.0)
        act(rs, rs, AF.Exp, [("rs",)], [("rs",)], scale=-0.5)
        tt(sq, oTf, rs, ALU.mult, [PK(B_O), PK(B_O + 1), ("rs",)], [PK(B_O), PK(B_O + 1), ("sq",)])
        stt(hgT[:, h, :], sq, r2T[:, 32:33], sog, ALU.mult, ALU.mult, [("sq",), "r2T", ("sog", par)], [("hgT", h)])

    def drive(gens):
        gens = [g for g in gens if g is not None]
        while gens:
            for g in list(gens):
                try:
                    next(g)
                except StopIteration:
                    gens.remove(g)

    if sub < 99:
        prep_common(0)
        drive([prep_dir(0, 0), prep_dir(0, 1)])
        if sub >= 2:
            prep_finish(0)
        if sub >= 3:
            drive([head_rec(0)])
        return finish()
    for h in range(H + 1):
        if h < H:
            prep_common(h)
        drive([prep_dir(h, 0) if h < H else None, prep_dir(h, 1) if h < H else None,
               head_rec(h - 1) if h >= 1 else None])
        if h < H:
            prep_finish(h)
    NWc[0] = NW
    if "hgT" in dbg_d:
        for h in range(H):
            dump_bf("hgT", hgT[:, h, :], [("hgT", h)], T, (h * P, (h + 1) * P), R_SQ + (h % 2) * 1024)
    if stage < 3:
        return finish()
    S.barrier()
    pool_banks[0] = list(range(8))
    sp_dma(gsg[:], gsgu_d.partition_broadcast(P), [], ["gsg"], "c6")

    uT = cb(0, H * 1024).rearrange("p (g c) -> p g c", g=H)
    vg = cf(8192, NT * 1024).rearrange("p (t c) -> p t c", t=NT)
    vn = cb(16384, NT * 1024).rearrange("p (t c) -> p t c", t=NT)
    sguT = cb(20480, H * 1024).rearrange("p (g c) -> p g c", g=H)
    junk3 = cb(24576, 1024)
    for cg in range(2):
        wi = load_in(C_ZU + cg * 512)
        for hh in range(4):
            g = cg * 4 + hh
            for half in range(2):
                b = nb()
                proj_fm(wi, hh, half, b)
                act(uT[:, g, half * 512:(half + 1) * 512], bank(b), AF.Gelu, [PK(b)], [PK(b), ("uT", g)])
    for cg in range(2):
        wi = load_in(C_ZV + cg * 512)
        for t_i in range(NT):
            b = nb()
            proj_tm(wi, t_i, b)
            act(vg[:, t_i, cg * 512:(cg + 1) * 512], bank(b), AF.Gelu, [PK(b)], [PK(b), ("vg", t_i), ("st", t_i)],
                accum=st[:, 4 * t_i + cg:4 * t_i + cg + 1])
    for t_i in range(NT):
        kS = ("st", t_i)
        c_s0 = st[:, 4 * t_i:4 * t_i + 1]
        c_s1 = st[:, 4 * t_i + 1:4 * t_i + 2]
        c_q = st[:, 4 * t_i + 2:4 * t_i + 3]
        c_m = st[:, 4 * t_i + 3:4 * t_i + 4]
        act(junk3, vg[:, t_i, :], AF.Square, [("vg", t_i)], [("junk3",), kS], accum=c_q)
        tt(c_m, c_s0, c_s1, ALU.add, [kS], [kS])
        ts(c_m, c_m, 1.0 / 1024, None, ALU.mult, r=[kS], w=[kS])
        ts(c_q, c_q, 1.0 / 1024, None, ALU.mult, r=[kS], w=[kS])
        stt(c_q, c_m, c_m, c_q, ALU.mult, ALU.subtract, [kS], [kS])
        act(c_q, c_q, AF.Ln, [kS, "epsc"], [kS], bias=epsc, scale=-1.0)
        act(c_q, c_q, AF.Exp, [kS], [kS], scale=-0.5)
        ts(c_m, c_m, c_q, None, ALU.mult, r=[kS], w=[kS])
        ts(c_m, c_m, -1.0, None, ALU.mult, r=[kS], w=[kS])
        act(vg[:, t_i, :], vg[:, t_i, :], AF.Identity, [("vg", t_i), kS], [("vg", t_i)], bias=c_m, scale=c_q)
        tt(vn[:, t_i, :], vg[:, t_i, :], gsg[:], ALU.mult, [("vg", t_i), "gsg"], [("vn", t_i)])
    for g in range(H):
        for half in range(2):
            b = nb()
            for q in range(4):
                t_i = half * 4 + q
                mm(bank(b)[:, q * P:(q + 1) * P], vn[:, t_i, g * P:(g + 1) * P], wsT[:, g, :], q == 0, False,
                   r=[("vn", t_i), "wsT"], w=[PK(b)])
                mm(bank(b)[:, q * P:(q + 1) * P], ones1[0:1, :], bsp[0:1, g * P:(g + 1) * P], False, q == 3,
                   r=["ones1", "bsp"], w=[PK(b)])
            hs = slice(half * 512, (half + 1) * 512)
            tt(sguT[:, g, hs], bank(b), uT[:, g, hs], ALU.mult, [PK(b), ("uT", g)], [PK(b), ("sguT", g)])
    if "sguT" in dbg_d:
        for g in range(H):
            dump_bf("sguT", sguT[:, g, :], [("sguT", g)], T, (g * P, (g + 1) * P), 8192 + (g % 2) * 1024)
    if stage < 4:
        return finish()
    S.barrier()

    mergedT = cb(8192, KC * 1024).rearrange("p (k c) -> p k c", k=KC)
    t12 = [cf(i * 512, 512) for i in range(4)]
    mi = 0
    for dg in range(4):
        wga = load_in(C_GA + dg * 512)
        wgb = load_in(C_GB + dg * 512)
        wab = load_t(wab_d[dg])
        for hh in range(4):
            dch = dg * 4 + hh
            for half in range(2):
                hs = slice(half * 512, (half + 1) * 512)
                bga = nb()
                proj_fm(wga, hh, half, bga)
                bgb = nb()
                proj_fm(wgb, hh, half, bgb)
                bya = nb()
                for kc in range(8):
                    mm(bank(bya), wb[wab][:, kc, hh * P:(hh + 1) * P], hgT[:, kc, hs], kc == 0, kc == 7,
                       r=[("wb", wab), ("hgT", kc)], w=[PK(bya)])
                byb = nb()
                for kc in range(8):
                    mm(bank(byb), wb[wab][:, 8 + kc, hh * P:(hh + 1) * P], sguT[:, kc, hs], kc == 0, kc == 7,
                       r=[("wb", wab), ("sguT", kc)], w=[PK(byb)])
                t1 = t12[(mi % 2) * 2]
                t2 = t12[(mi % 2) * 2 + 1]
                k1 = ("t12", (mi % 2) * 2)
                k2 = ("t12", (mi % 2) * 2 + 1)
                mi += 1
                act(t1, bank(bga), AF.Sigmoid, [PK(bga)], [PK(bga), k1])
                act(t2, bank(bgb), AF.Sigmoid, [PK(bgb)], [PK(bgb), k2])
                tt(t1, bank(bya), t1, ALU.mult, [PK(bya), k1], [PK(bya), k1])
                tt(t2, bank(byb), t2, ALU.mult, [PK(byb), k2], [PK(byb), k2])
                tt(mergedT[:, dch, hs], t1, t2, ALU.add, [k1, k2], [("mg", dch, half)])
    if "mergedT" in dbg_d:
        for k in range(KC):
            dump_bf("mergedT", mergedT[:, k, :], [("mg", k, 0), ("mg", k, 1)], T, (k * P, (k + 1) * P), 16384 + (k % 2) * 1024)
    if stage < 5 and sub == 40:
        return finish()
    S.barrier()

    xts = [cf(i * D, D) for i in range(2)]
    gg1 = cf(2 * D, D)
    gtmp = cf(3 * D, D)
    yh = cf(16384, 4 * D).rearrange("p (q c) -> p q c", q=4)
    junk4 = cb(24576, D)
    modflat = modg_d.rearrange("r c -> (r c)")

    def load_gg(dst, tmpb, which, key):
        sp_dma(dst, modflat[(2 + 3 * which) * D:(3 + 3 * which) * D].partition_broadcast(P), ["modg"], [key], "gg")
        sp_dma(tmpb, gpost_d[which, :].partition_broadcast(P), [], [(key, "t")], "ggt")
        tt(dst, dst, tmpb, ALU.mult, [key, (key, "t")], [key])
    load_gg(gg1, gtmp, 0, ("gg", 0))

    def rstd_from4(q, kS):
        c = [st[:, 16 + 4 * q + i:16 + 4 * q + i + 1] for i in range(4)]
        tt(c[0], c[0], c[1], ALU.add, [kS], [kS])
        tt(c[2], c[2], c[3], ALU.add, [kS], [kS])
        tt(c[0], c[0], c[2], ALU.add, [kS], [kS])
        act(c[0], c[0], AF.Ln, [kS, "epsc"], [kS], bias=epsc, scale=1.0 / D)
        act(c[0], c[0], AF.Exp, [kS], [kS], scale=-0.5)
        return c[0]

    xcnt = [0]

    def wo_half(half):
        for cbk in range(4):
            wi = load_t(wo_d[cbk])
            for q in range(4):
                t_i = half * 4 + q
                b = nb()
                for kc in range(KC):
                    mm(bank(b), mergedT[:, kc, t_i * P:(t_i + 1) * P], wb[wi][:, kc, :], kc == 0, kc == KC - 1,
                       r=[("wb", wi), ("mg", kc, half)], w=[PK(b)])
                cp(yh[:, q, cbk * 512:(cbk + 1) * 512], bank(b), [PK(b)], [PK(b), ("yh", q)])
                act(junk4[:, 0:512], bank(b), AF.Square, [PK(b)], [PK(b), ("junk4",), ("st4", q)],
                    accum=st[:, 16 + 4 * q + cbk:16 + 4 * q + cbk + 1])

    def get_h(t_i):
        half, q = t_i // 4, t_i % 4
        if q == 0:
            wo_half(half)
        kS = ("st4", q)
        rs = rstd_from4(q, kS)
        xi = xcnt[0] % 2
        xcnt[0] += 1
        sp_dma(xts[xi], x_d[t_i * P:(t_i + 1) * P, :], [], [("xts", xi)], f"xts{xi}")
        stt(yh[:, q, :], yh[:, q, :], rs, gg1, ALU.mult, ALU.mult, [("yh", q), kS, ("gg", 0)], [("yh", q)])
        tt(yh[:, q, :], yh[:, q, :], xts[xi], ALU.add, [("yh", q), ("xts", xi)], [("yh", q)])
        sp_dma(out_d[t_i * P:(t_i + 1) * P, :], yh[:, q, :], [("yh", q)], [("out", t_i)], "hsp")
        if "h" in dbg_d:
            finals.append(sp_dma(dbg_d["h"][t_i * P:(t_i + 1) * P, :], yh[:, q, :], [("yh", q)], [("dbgh", t_i)], "dbg_h"))
        return yh[:, q, :], [("yh", q)]

    norm_to_aT(get_h, 24576, 1)
    if "a2T" in dbg_d:
        for kc in range(KC):
            dump_bf("a2T", aT[:, kc, :], [("aT", kc, 0), ("aT", kc, 1)], T, (kc * P, (kc + 1) * P), (kc % 2) * T)
    if stage < 5:
        return finish()
    S.barrier()

    wcount[0] = 0
    NWc[0] = 2
    w2f = wb[2][:].rearrange("p k c -> p (k c)").bitcast(F32)
    gg2 = w2f[:, 0:D]
    hb = w2f[:, D:2 * D]
    hid = cb(0, 64 * 512).rearrange("p (k c) -> p k c", k=64)
    y2 = cf(16384, 4 * D).rearrange("p (q c) -> p q c", q=4)
    rts = [cf(24576 + i * 512, 512) for i in range(2)]
    junk5 = cb(25600, 512)
    load_gg(gg2, y2[:, 0, :], 1, ("gg", 1))
    ri = 0
    for half in range(2):
        hs = slice(half * 512, (half + 1) * 512)
        pool_banks[0] = list(range(8))
        for hg16 in range(16):
            wi = load_t(wff1_d[hg16])
            for hh in range(4):
                hc = hg16 * 4 + hh
                b = nb()
                proj_fm(wi, hh, half, b)
                rt = rts[ri % 2]
                kR = ("rt", ri % 2)
                ri += 1
                act(rt, bank(b), AF.Relu, [PK(b)], [PK(b), kR])
                tt(hid[:, hc, :], rt, rt, ALU.mult, [kR], [("hid", hc)])
        for cbp in range(2):
            for kg in range(8):
                wi = load_t(wff2_d[cbp * 8 + kg])
                w2v = wb[wi][:].rearrange("p (a b) c -> p a (b c)", a=8)
                for hc8 in range(8):
                    hc = kg * 8 + hc8
                    for q in range(4):
                        for cb2 in range(2):
                            bnk = q * 2 + cb2
                            mm(bank(bnk), hid[:, hc, q * P:(q + 1) * P], w2v[:, hc8, cb2 * 512:(cb2 + 1) * 512],
                               hc == 0, hc == 63, r=[("hid", hc), ("wb", wi)], w=[PK(bnk)])
            for q in range(4):
                for cb2 in range(2):
                    bnk = q * 2 + cb2
                    col = cbp * 1024 + cb2 * 512
                    cp(y2[:, q, col:col + 512], bank(bnk), [PK(bnk)], [PK(bnk), ("y2", q)])
                    act(junk5, bank(bnk), AF.Square, [PK(bnk)], [PK(bnk), ("junk5",), ("st5", q)],
                        accum=st[:, 16 + 4 * q + cbp * 2 + cb2:16 + 4 * q + cbp * 2 + cb2 + 1])
        for q in range(4):
            t_i = half * 4 + q
            kS = ("st5", q)
            rs = rstd_from4(q, kS)
            sp_dma(hb, out_d[t_i * P:(t_i + 1) * P, :], [("out", t_i)], [("hb",)], "hb")
            stt(y2[:, q, :], y2[:, q, :], rs, gg2, ALU.mult, ALU.mult, [("y2", q), kS, ("gg", 1)], [("y2", q)])
            tt(y2[:, q, :], y2[:, q, :], hb, ALU.add, [("y2", q), ("hb",)], [("y2", q)])
            finals.append(sp_dma(out_d[t_i * P:(t_i + 1) * P, :], y2[:, q, :], [("y2", q), ("hb",)], [("out", t_i)], "ost"))
    return finish()


_CACHE = {}


def make_in_maps(inputs, stage=99):
    f = lambda a: np.ascontiguousarray(np.asarray(a, dtype=np.float32))
    x = f(inputs["x"]); c = f(inputs["c"])
    w_ada = f(inputs["w_ada"])[0]; b_ada = f(inputs["b_ada"])[0]
    lb = f(inputs["lb_logits"])
    cstm = np.zeros((P, 4 * P), np.float32)
    cstm[:, 0:P] = np.eye(P, dtype=np.float32)
    s_idx = np.arange(P)[:, None]
    c_idx = np.arange(P)[None, :]
    same = (s_idx // CH) == (c_idx // CH)
    cstm[:, P:2 * P] = (same & (s_idx <= c_idx)).astype(np.float32)
    cstm[:, 2 * P:3 * P] = (same & (s_idx >= c_idx)).astype(np.float32)
    cstm[:, 3 * P:4 * P] = 1.0 / P
    rows = np.zeros((P, P), np.float32)
    rows[96:112] = f(inputs["g_pre_mix"])[0].reshape(16, P)
    rows[112:128] = f(inputs["g_pre_ffn"])[0].reshape(16, P)
    rows2 = np.zeros((P, P), np.float32)
    rows2[0:32] = lb.reshape(32, P)
    rows2[32] = f(inputs["g_hgrn_norm"])[0]
    gpost = np.stack([f(inputs["g_post_mix"])[0], f(inputs["g_post_ffn"])[0]])
    def tile_std(W):
        K_, N_ = W.shape
        return np.ascontiguousarray(W.reshape(K_ // P, P, N_ // 512, 512).transpose(2, 1, 0, 3)).reshape(N_ // 512, P, (K_ // P) * 512)
    shared = {
        "rows": rows, "rows2": rows2, "gpost": gpost,
        "gsgu": f(inputs["g_sgu_norm"]), "bsp": f(inputs["b_spatial"])[0].reshape(1, 1024),
        "wsp": f(inputs["w_spatial"])[0], "cst": cstm,
    }
    if stage >= 2:
        w_in = f(inputs["w_in"])[0]
        w4 = w_in.reshape(KC, P, INW // 512, 512)
        shared["w_in_std"] = np.ascontiguousarray(w4[:, :, STD_TILES, :].transpose(2, 1, 0, 3)).reshape(len(STD_TILES), P, KC * 512)
        wh = np.stack([w_in[:, c0:c0 + 1024].reshape(KC, P, H, P) for c0 in (C_Q, C_FF, C_FB, C_OG)], axis=3)
        shared["w_in_head"] = np.ascontiguousarray(wh.transpose(2, 1, 0, 3, 4)).reshape(H, P, KC * 512)
    if stage >= 4:
        wab = np.concatenate([f(inputs["w_a_out"])[0], f(inputs["w_b_out"])[0]], axis=0)
        shared["w_ab_out"] = tile_std(wab)
        shared["w_o"] = tile_std(f(inputs["w_o"])[0])
    if stage >= 5:
        shared["w_ff1"] = tile_std(f(inputs["w_ff1"])[0])
        w2 = f(inputs["w_ff2"])[0].reshape(8, 8, P, 2, 1024)
        shared["w_ff2"] = np.ascontiguousarray(w2.transpose(3, 0, 2, 1, 4)).reshape(16, P, 8 * 1024)
    small = np.zeros((1, 1, 1), np.float32)
    for name, st_min in (("w_in_std", 2), ("w_in_head", 2), ("w_ab_out", 4), ("w_o", 4), ("w_ff1", 5), ("w_ff2", 5)):
        if stage < st_min:
            shared[name] = small
    maps = []
    for core in range(8):
        b, j = core // 4, core % 4
        m = dict(shared)
        m["x"] = np.ascontiguousarray(x[b, j * T:(j + 1) * T])
        m["cT"] = np.ascontiguousarray(c[b].reshape(KC, P).T)
        m["w_ada"] = tile_std(w_ada[:, j * 3072:(j + 1) * 3072])
        m["b_ada"] = np.ascontiguousarray(b_ada[j * 3072:(j + 1) * 3072].reshape(1, 3072))
        pm = np.zeros((P, 8), np.float32)
        for i in range(4):
            pm[:, i] = 1.0 if i < j else 0.0
            pm[:, 4 + i] = 1.0 if i > j else 0.0
        m["posm"] = pm
        maps.append(m)
    return maps


def kernel(**inputs):
    if "nc" not in _CACHE:
        _CACHE["nc"] = build()
    nc, _stack = _CACHE["nc"]
    maps = make_in_maps(inputs)
    res = run_bass_kernel_spmd(nc, maps, core_ids=list(range(8)))
    out = np.zeros((2, 4096, D), np.float32)
    for core in range(8):
        b, j = core // 4, core % 4
        out[b, j * T:(j + 1) * T] = res.results[core]["out"]
    return out
```
